# Optimizing a Trainium2 kernel written in Bass

```python
import math
import jax, jax.numpy as jnp
from jax import lax
import numpy as np

D_MODEL = 4096
BATCH = 1
SEQ = 8192
DEPTH = 1

CHUNK = 64
MIX_WIDTH = D_MODEL
SSM_WIDTH = MIX_WIDTH // 2
SSM_GROUP = 16
SSM_GROUPS = SSM_WIDTH // SSM_GROUP
SSM_STATE = 64
ATT_QK_DIM = 128
ATT_V_DIM = 2 * ATT_QK_DIM
ATT_WIDTH = MIX_WIDTH - SSM_WIDTH
ATT_HEADS = ATT_WIDTH // ATT_V_DIM
QK_WIDTH = ATT_HEADS * 2 * ATT_QK_DIM
IN_WIDTH = SSM_WIDTH + 2 * QK_WIDTH + ATT_WIDTH
D_FF = ((8 * D_MODEL // 3 + 255) // 256) * 256
CONV_WIDTH = 3
REL_BUCKETS = 32
REL_MAX_DIST = 128
Q_BLOCK = 128
ALPHA = (2 * DEPTH) ** 0.25
BETA = (8 * DEPTH) ** -0.25
LN_EPS = 1e-5
NEG_INF = -1e30

kernel_name = 'hybrid_s5_diffattn_convffn_deepnorm'


def layer_norm(x, g, b):
    xf = x.astype(jnp.float32)
    mu = jnp.mean(xf, axis=-1, keepdims=True)
    xc = xf - mu
    var = jnp.mean(xc * xc, axis=-1, keepdims=True)
    return (xc * lax.rsqrt(var + LN_EPS) * g.astype(jnp.float32) + b.astype(jnp.float32)).astype(x.dtype)


def ssm_mixer(u, log_step, lam_re, lam_im, b_re, b_im, c_re, c_im, d_skip, w_glu, b_glu):
    f32 = jnp.float32
    bsz, seq, _ = u.shape
    ug = u.reshape(bsz, seq, SSM_GROUPS, SSM_GROUP).astype(f32)
    step = jnp.exp(log_step.astype(f32))[:, None]
    lr = lam_re.astype(f32)
    li = lam_im.astype(f32)
    mag = jnp.exp(lr * step)
    ab_re = mag * jnp.cos(li * step)
    ab_im = mag * jnp.sin(li * step)
    den = lr * lr + li * li
    nr = ab_re - 1.0
    ni = ab_im
    z_re = (nr * lr + ni * li) / den
    z_im = (ni * lr - nr * li) / den
    br = b_re.astype(f32)
    bi = b_im.astype(f32)
    bb_re = z_re[..., None] * br - z_im[..., None] * bi
    bb_im = z_re[..., None] * bi + z_im[..., None] * br
    bu_re = jnp.einsum('bsgh,gnh->bsgn', ug, bb_re)
    bu_im = jnp.einsum('bsgh,gnh->bsgn', ug, bb_im)
    a_re = jnp.broadcast_to(ab_re, bu_re.shape)
    a_im = jnp.broadcast_to(ab_im, bu_im.shape)

    def combine(e1, e2):
        a1r, a1i, b1r, b1i = e1
        a2r, a2i, b2r, b2i = e2
        return (a2r * a1r - a2i * a1i,
                a2r * a1i + a2i * a1r,
                a2r * b1r - a2i * b1i + b2r,
                a2r * b1i + a2i * b1r + b2i)

    _, _, h_re, h_im = lax.associative_scan(combine, (a_re, a_im, bu_re, bu_im), axis=1)
    y = (jnp.einsum('ghn,bsgn->bsgh', c_re.astype(f32), h_re)
         - jnp.einsum('ghn,bsgn->bsgh', c_im.astype(f32), h_im))
    y = y + d_skip.astype(f32).reshape(SSM_GROUPS, SSM_GROUP) * ug
    y = jax.nn.gelu(y.reshape(bsz, seq, SSM_WIDTH).astype(u.dtype))
    return y * jax.nn.sigmoid(y @ w_glu + b_glu)


def t5_bucket(rel):
    half = REL_BUCKETS // 2
    max_exact = half // 2
    ret = jnp.where(rel > 0, half, 0)
    n = jnp.abs(rel)
    nf = jnp.maximum(n, 1).astype(jnp.float32)
    large = max_exact + (jnp.log(nf / max_exact) / math.log(REL_MAX_DIST / max_exact)
                         * (half - max_exact)).astype(jnp.int32)
    large = jnp.minimum(large, half - 1)
    return ret + jnp.where(n < max_exact, n, large)


def diff_attention(q, k, v, rel_bias, lam_q1, lam_k1, lam_q2, lam_k2, subln_g, lambda_init):
    f32 = jnp.float32
    bsz, seq = q.shape[0], q.shape[1]
    nb = seq // Q_BLOCK
    scale = ATT_QK_DIM ** -0.5
    lam = (jnp.exp(jnp.sum(lam_q1.astype(f32) * lam_k1.astype(f32)))
           - jnp.exp(jnp.sum(lam_q2.astype(f32) * lam_k2.astype(f32))) + lambda_init)
    kt = k.transpose(0, 2, 3, 1, 4)
    vt = v.transpose(0, 2, 1, 3)
    qb = (q * scale).reshape(bsz, nb, Q_BLOCK, ATT_HEADS, 2, ATT_QK_DIM).transpose(1, 0, 3, 4, 2, 5)
    kpos = jnp.arange(seq)

    def block(args):
        qblk, bidx = args
        qpos = bidx * Q_BLOCK + jnp.arange(Q_BLOCK)
        rel = kpos[None, :] - qpos[:, None]
        bias = rel_bias[t5_bucket(rel)].transpose(2, 0, 1).astype(f32)
        visible = (kpos[None, :] // CHUNK) <= (qpos[:, None] // CHUNK)
        s = jnp.einsum('bhcqd,bhckd->bhcqk', qblk, kt).astype(f32) + bias[None, :, None]
        s = jnp.where(visible, s, NEG_INF)
        p = jax.nn.softmax(s, axis=-1)
        attn = p[:, :, 0] - lam * p[:, :, 1]
        return jnp.einsum('bhqk,bhkd->bhqd', attn.astype(v.dtype), vt)

    o = lax.map(block, (qb, jnp.arange(nb)))
    o = o.transpose(1, 0, 3, 2, 4).reshape(bsz, seq, ATT_HEADS, ATT_V_DIM).astype(f32)
    o = o * lax.rsqrt(jnp.mean(o * o, axis=-1, keepdims=True) + LN_EPS) * subln_g.astype(f32)
    o = o * (1.0 - lambda_init)
    return o.reshape(bsz, seq, ATT_WIDTH).astype(v.dtype)


def conv_ffn(h, w_up, conv_w, conv_b, w_down):
    seq = h.shape[1]
    up = h @ w_up
    a, g = jnp.split(up, 2, axis=-1)
    gp = jnp.pad(g, ((0, 0), (CONV_WIDTH - 1, 0), (0, 0)))
    gc = conv_b
    for j in range(CONV_WIDTH):
        gc = gc + gp[:, j:j + seq] * conv_w[j]
    return (jax.nn.silu(gc) * a) @ w_down


def setup_inputs(seed: int = 0) -> dict:
    key = jax.random.key(seed)
    ks = jax.random.split(key, 32)
    f32 = jnp.float32
    L = DEPTH
    nrm = lambda k, shp, s: jax.random.normal(k, shp, f32) * s
    x = nrm(ks[0], (BATCH, SEQ, D_MODEL), 1.0)
    col_scale = jnp.concatenate([jnp.ones((SSM_WIDTH + 2 * QK_WIDTH,), f32),
                                 jnp.full((ATT_WIDTH,), BETA, f32)])
    w_in = nrm(ks[1], (L, D_MODEL, IN_WIDTH), D_MODEL ** -0.5) * col_scale
    ssm_log_step = jax.random.uniform(ks[2], (L, SSM_GROUPS), f32, math.log(1e-3), math.log(1e-1))
    ssm_lambda_re = -0.5 + nrm(ks[3], (L, SSM_GROUPS, SSM_STATE), 0.01)
    ssm_lambda_im = jnp.broadcast_to(math.pi * jnp.arange(SSM_STATE, dtype=f32), (L, SSM_GROUPS, SSM_STATE)) \
        + nrm(ks[4], (L, SSM_GROUPS, SSM_STATE), 0.01)
    ssm_b_re = nrm(ks[5], (L, SSM_GROUPS, SSM_STATE, SSM_GROUP), (2 * SSM_GROUP) ** -0.5)
    ssm_b_im = nrm(ks[6], (L, SSM_GROUPS, SSM_STATE, SSM_GROUP), (2 * SSM_GROUP) ** -0.5)
    ssm_c_re = nrm(ks[7], (L, SSM_GROUPS, SSM_GROUP, SSM_STATE), (2 * SSM_STATE) ** -0.5)
    ssm_c_im = nrm(ks[8], (L, SSM_GROUPS, SSM_GROUP, SSM_STATE), (2 * SSM_STATE) ** -0.5)
    ssm_d = nrm(ks[9], (L, SSM_WIDTH), 1.0)
    ssm_w_glu = nrm(ks[10], (L, SSM_WIDTH, SSM_WIDTH), SSM_WIDTH ** -0.5)
    ssm_b_glu = nrm(ks[11], (L, SSM_WIDTH), 0.02)
    att_lambda_q1 = nrm(ks[12], (L, ATT_QK_DIM), 0.1)
    att_lambda_k1 = nrm(ks[13], (L, ATT_QK_DIM), 0.1)
    att_lambda_q2 = nrm(ks[14], (L, ATT_QK_DIM), 0.1)
    att_lambda_k2 = nrm(ks[15], (L, ATT_QK_DIM), 0.1)
    att_subln_g = 1.0 + nrm(ks[16], (L, ATT_V_DIM), 0.02)
    rel_bias = nrm(ks[17], (REL_BUCKETS, ATT_HEADS), 0.5)
    w_out = nrm(ks[18], (L, MIX_WIDTH, D_MODEL), MIX_WIDTH ** -0.5 * BETA)
    ln1_g = 1.0 + nrm(ks[19], (L, D_MODEL), 0.02)
    ln1_b = nrm(ks[20], (L, D_MODEL), 0.02)
    ffn_w_up = nrm(ks[21], (L, D_MODEL, 2 * D_FF), D_MODEL ** -0.5)
    ffn_conv_w = nrm(ks[22], (L, CONV_WIDTH, D_FF), CONV_WIDTH ** -0.5)
    ffn_conv_b = nrm(ks[23], (L, D_FF), 0.02)
    ffn_w_down = nrm(ks[24], (L, D_FF, D_MODEL), D_FF ** -0.5 * BETA)
    ln2_g = 1.0 + nrm(ks[25], (L, D_MODEL), 0.02)
    ln2_b = nrm(ks[26], (L, D_MODEL), 0.02)
    return {'x': x, 'w_in': w_in, 'ssm_log_step': ssm_log_step, 'ssm_lambda_re': ssm_lambda_re,
            'ssm_lambda_im': ssm_lambda_im, 'ssm_b_re': ssm_b_re, 'ssm_b_im': ssm_b_im,
            'ssm_c_re': ssm_c_re, 'ssm_c_im': ssm_c_im, 'ssm_d': ssm_d, 'ssm_w_glu': ssm_w_glu,
            'ssm_b_glu': ssm_b_glu, 'att_lambda_q1': att_lambda_q1, 'att_lambda_k1': att_lambda_k1,
            'att_lambda_q2': att_lambda_q2, 'att_lambda_k2': att_lambda_k2, 'att_subln_g': att_subln_g,
            'rel_bias': rel_bias, 'w_out': w_out, 'ln1_g': ln1_g, 'ln1_b': ln1_b,
            'ffn_w_up': ffn_w_up, 'ffn_conv_w': ffn_conv_w, 'ffn_conv_b': ffn_conv_b,
            'ffn_w_down': ffn_w_down, 'ln2_g': ln2_g, 'ln2_b': ln2_b}


def reference(x, w_in, ssm_log_step, ssm_lambda_re, ssm_lambda_im, ssm_b_re, ssm_b_im,
              ssm_c_re, ssm_c_im, ssm_d, ssm_w_glu, ssm_b_glu, att_lambda_q1, att_lambda_k1,
              att_lambda_q2, att_lambda_k2, att_subln_g, rel_bias, w_out, ln1_g, ln1_b,
              ffn_w_up, ffn_conv_w, ffn_conv_b, ffn_w_down, ln2_g, ln2_b):
    h = x
    bsz, seq = x.shape[0], x.shape[1]
    for l in range(DEPTH):
        lambda_init = 0.8 - 0.6 * math.exp(-0.3 * l)
        proj = h @ w_in[l]
        u, q, k, v = jnp.split(proj, [SSM_WIDTH, SSM_WIDTH + QK_WIDTH, SSM_WIDTH + 2 * QK_WIDTH], axis=-1)
        q = q.reshape(bsz, seq, ATT_HEADS, 2, ATT_QK_DIM)
        k = k.reshape(bsz, seq, ATT_HEADS, 2, ATT_QK_DIM)
        v = v.reshape(bsz, seq, ATT_HEADS, ATT_V_DIM)
        y_ssm = ssm_mixer(u, ssm_log_step[l], ssm_lambda_re[l], ssm_lambda_im[l], ssm_b_re[l],
                          ssm_b_im[l], ssm_c_re[l], ssm_c_im[l], ssm_d[l], ssm_w_glu[l], ssm_b_glu[l])
        y_att = diff_attention(q, k, v, rel_bias, att_lambda_q1[l], att_lambda_k1[l],
                               att_lambda_q2[l], att_lambda_k2[l], att_subln_g[l], lambda_init)
        mix = jnp.concatenate([y_ssm, y_att], axis=-1) @ w_out[l]
        h = layer_norm(ALPHA * h + mix, ln1_g[l], ln1_b[l])
        ff = conv_ffn(h, ffn_w_up[l], ffn_conv_w[l], ffn_conv_b[l], ffn_w_down[l])
        h = layer_norm(ALPHA * h + ff, ln2_g[l], ln2_b[l])
    return h
```

```python
import math
from contextlib import ExitStack
import numpy as np
import concourse.bass as bass
import concourse.mybir as mybir
from concourse.bass_utils import run_bass_kernel_spmd

F32 = mybir.dt.float32
BF16 = mybir.dt.bfloat16
AF = mybir.ActivationFunctionType
ALU = mybir.AluOpType
AX = mybir.AxisListType

NCORE = 8
S = 8192
D = 4096
DFF = 11008
PADC = 2
ALPHA = 2.0 ** 0.25
EPS = 1e-5
LAMBDA_INIT = 0.8 - 0.6 * math.exp(0.0)
TT = 512
NT = TT + 2
NTP = 520
NTILE = 1024 // TT
PI = math.pi


DECLARED = []


class Tracker:
    def __init__(self, nc, es):
        self.nc = nc
        self.es = es
        self.eng = {"pe": nc.tensor, "act": nc.scalar, "dve": nc.vector, "pool": nc.gpsimd, "sp": nc.sync}
        self.sem = {}
        self.cnt = {}
        self.waited = {}
        self.writers = {}
        self.readers = {}
        for e in ("pe", "act", "dve", "pool"):
            self._mk("E_" + e)

    def _mk(self, name):
        if name not in self.sem:
            self.sem[name] = self.es.enter_context(self.nc.semaphore(name))
            self.cnt[name] = 0
        return name

    def _wait(self, e, tok):
        sem, cnt = tok
        if self.waited.get((e, sem), 0) >= cnt:
            return
        self.eng[e].wait_ge(self.sem[sem], cnt)
        self.waited[(e, sem)] = cnt

    def _deps(self, e, reads, writes):
        own = "E_" + e
        for r in reads:
            for sem, cnt in self.writers.get(r, {}).items():
                self._wait(e, (sem, cnt))
        for w in writes:
            for sem, cnt in self.writers.get(w, {}).items():
                if sem == own:
                    continue
                self._wait(e, (sem, cnt))
            for sem, cnt in self.readers.get(w, {}).items():
                if sem == own:
                    continue
                self._wait(e, (sem, cnt))

    def _record(self, tok, reads, writes):
        sem, cnt = tok
        for r in reads:
            self.readers.setdefault(r, {})[sem] = cnt
        for w in writes:
            self.writers.setdefault(w, {})[sem] = cnt

    def op(self, e, fn, reads=(), writes=()):
        self._deps(e, reads, writes)
        ins = fn()
        sem = "E_" + e
        self.cnt[sem] += 1
        ins.then_inc(self.sem[sem], 1)
        self._record((sem, self.cnt[sem]), reads, writes)

    def group(self, e, fns, reads=(), writes=()):
        self._deps(e, reads, writes)
        ins = None
        for f in fns:
            ins = f()
        sem = "E_" + e
        self.cnt[sem] += 1
        ins.then_inc(self.sem[sem], 1)
        self._record((sem, self.cnt[sem]), reads, writes)

    def dma(self, q, out, in_, reads=(), writes=(), slot=None):
        self._deps(q, reads, writes)
        sem = self._mk("D_" + str(slot))
        ins = self.eng[q].dma_start(out=out, in_=in_)
        self.cnt[sem] += 16
        ins.then_inc(self.sem[sem], 16)
        self._record((sem, self.cnt[sem]), reads, writes)

    def barrier(self):
        for e in self.eng:
            for sem, cnt in self.cnt.items():
                if cnt > 0:
                    self._wait(e, (sem, cnt))
        self.writers.clear()
        self.readers.clear()


_UID = [0]


def _sb(ph, nc, name, shape, dt):
    _UID[0] += 1
    return ph.enter_context(nc.sbuf_tensor(f"sb_{name}_{_UID[0]}", list(shape), dt))


def _ps(ph, nc, name, shape, dt=F32):
    _UID[0] += 1
    return ph.enter_context(nc.psum_tensor(f"pp_{name}_{_UID[0]}", list(shape), dt))


def gemm(T, nc, tag, W, K, cols, act, actkey, tok, epi, kg=16):
    KC = K // 128
    ngr = (KC + kg - 1) // kg
    with ExitStack() as ph:
        st = [_sb(ph, nc, f"{tag}_st{i}", [128, kg, 128], F32) for i in range(2)]
        wb = [_sb(ph, nc, f"{tag}_wb{i}", [128, kg, 128], BF16) for i in range(2)]
        ps = [[_ps(ph, nc, f"{tag}_ps{i}_{j}", [128, 512]) for j in range(len(tok))] for i in range(2)]
        it = 0
        for mi, c0 in enumerate(cols):
            pss = ps[mi % 2]
            for gi in range(ngr):
                k0 = gi * kg
                n = min(kg, KC - k0)
                s = it % 2
                it += 1
                src = W[k0 * 128:(k0 + n) * 128, c0:c0 + 128].rearrange("(kc p) m -> p kc m", p=128)
                T.dma("sp", st[s][:, 0:n, :], src, writes=[(tag, "st", s)], slot=f"{tag}st{s}")
                ce = "dve" if (it % 2 == 0) else "pool"
                T.op(ce, lambda s=s, n=n, ce=ce: T.eng[ce].tensor_copy(wb[s][:, 0:n, :], st[s][:, 0:n, :]),
                     reads=[(tag, "st", s)], writes=[(tag, "wb", s)])
                fns = []
                for j in range(n):
                    for si, (t0, ln) in enumerate(tok):
                        fns.append(lambda j=j, si=si, t0=t0, ln=ln, s=s, k0=k0: nc.tensor.matmul(
                            pss[si][:, 0:ln], wb[s][:, j, :], act[:, k0 + j, t0:t0 + ln],
                            start=(k0 + j == 0), stop=(k0 + j == KC - 1)))
                T.group("pe", fns, reads=[(tag, "wb", s), actkey], writes=[(tag, "ps", mi % 2)])
            epi(mi, pss, (tag, "ps", mi % 2))
        T.barrier()


def build(debug=False, phase_c=True, stop_after=9, pc_level=3, skip_a=False, sub=9):
    nc = bass.Bass("TRN2", target_bir_lowering=False)
    dr = {}
    DECLARED.clear()

    def din(name, shape, dt=F32):
        dr[name] = nc.dram_tensor(name, list(shape), dt, kind="ExternalInput")
        DECLARED.append(name)
        return dr[name]

    xT = din("xT", [D, S + PADC])
    w_in = din("w_in", [D, 1024])
    ident_in = din("ident", [128, 128])
    jmat_in = din("jmat", [128, 128])
    lrA = din("lrA", [16, 1024]); liA = din("liA", [16, 1024]); lsA = din("lsA", [16, 1024])
    brA = din("brA", [16, 1024]); biA = din("biA", [16, 1024])
    lrB = din("lrB", [128, 16]); liB = din("liB", [128, 16]); lsB = din("lsB", [128, 16])
    cT = din("cT", [128, 256])
    dT = din("dT", [16, 16])
    lq1 = din("lq1", [128, 128]); lk1 = din("lk1", [128, 128]); lq2 = din("lq2", [128, 128]); lk2 = din("lk2", [128, 128])
    gsub = din("gsub", [128, 256])
    cfar = din("cfar", [128, 1])
    bnear = din("bnear", [3, 128, 512])
    if phase_c:
        w_glu = din("w_glu", [2048, 2048]); b_glu = din("b_glu", [128, 16])
        w_out = din("w_out", [D, D])
        ln1g = din("ln1g", [128, 32]); ln1b = din("ln1b", [128, 32])
        if pc_level >= 3:
            w_up = din("w_up", [D, 2 * DFF])
            w_dn = din("w_dn", [DFF, D])
        cw = din("cw", [128, 3 * 86]); cb = din("cb", [128, 86])
        ln2g = din("ln2g", [128, 32]); ln2b = din("ln2b", [128, 32])
        hm = din("hm", [128, NTILE])
        oh = din("oh", [128, NCORE])
        xTc = din("xTc", [D, 1024 + PADC])
        outT = nc.dram_tensor("outT", [D, 1024], F32, kind="ExternalOutput") if pc_level >= 3 else None
    projT = nc.dram_tensor("projT", [1024, S], F32)
    ag_in = nc.dram_tensor("ag_in", [512, S + PADC], BF16)
    ag_out = nc.dram_tensor("ag_out", [NCORE * 512, S + PADC], BF16)
    if phase_c:
        agsel = nc.dram_tensor("agsel", [NCORE * 512, 1024 + PADC], BF16)
        h1s = nc.dram_tensor("h1s", [D, TT], F32)
        dbgc = None
        if pc_level == 1:
            dbgc = nc.dram_tensor("dbg_sel", [NCORE * 512, 1024 + PADC], BF16, kind="ExternalOutput")
        if pc_level == 2:
            dbgc = nc.dram_tensor("dbg_h1", [D, TT], F32, kind="ExternalOutput")
        p2s = nc.dram_tensor("p2s", [D, TT], F32)
    if debug:
        dbg_proj = nc.dram_tensor("dbg_proj", [1024, S], F32, kind="ExternalOutput")
        dbg_ag = nc.dram_tensor("dbg_ag", [512, S + PADC], BF16, kind="ExternalOutput")

    with ExitStack() as es:
        es.enter_context(nc.Block())
        T = Tracker(nc, es)
        V = nc.vector
        A = nc.scalar
        cst = ExitStack()
        es.enter_context(cst)
        ident = _sb(cst, nc, "ident", [128, 128], F32)
        identb = _sb(cst, nc, "identb", [128, 128], BF16)
        ones = _sb(cst, nc, "ones", [128, 128], F32)
        negpi = _sb(cst, nc, "negpi", [128, 1], F32)
        zero = _sb(cst, nc, "zero", [128, 8], BF16)
        T.dma("sp", ident[:], ident_in[:, :], writes=["ident"], slot="c0")
        T.op("dve", lambda: V.tensor_copy(identb[:], ident[:]), reads=["ident"], writes=["identb"])
        T.op("dve", lambda: V.memset(ones[:], 1.0), writes=["ones"])
        T.op("dve", lambda: V.memset(negpi[:], -PI), writes=["negpi"])
        T.op("dve", lambda: V.memset(zero[:], 0.0), writes=["zero"])
        T.dma("sp", ag_in[0:128, 0:PADC], zero[:, 0:PADC], reads=["zero"], slot="z0")
        T.dma("sp", ag_in[128:256, 0:PADC], zero[:, 0:PADC], reads=["zero"], slot="z0")
        T.dma("sp", ag_in[256:384, 0:PADC], zero[:, 0:PADC], reads=["zero"], slot="z0")
        T.dma("sp", ag_in[384:512, 0:PADC], zero[:, 0:PADC], reads=["zero"], slot="z0")

        if skip_a:
            T.barrier()
            build_phase_c(nc, T, dr, ag_out, agsel, h1s, p2s, outT, ones, pc_level, dbgc, sub)
            T.barrier()
            return nc
        with ExitStack() as ph:
            xb = _sb(ph, nc, "p1_xb", [128, 32, 1024], BF16)
            xs = [_sb(ph, nc, f"p1_xs{i}", [128, 1024], F32) for i in range(2)]
            osb = [_sb(ph, nc, f"p1_o{i}", [128, 1024], F32) for i in range(2)]
            for tt in range(S // 1024):
                for kc in range(32):
                    s = kc % 2
                    T.dma("pool", xs[s][:], xT[kc * 128:(kc + 1) * 128, PADC + tt * 1024:PADC + (tt + 1) * 1024],
                          writes=[("xs", s)], slot=f"xs{s}")
                    T.op("act", lambda s=s, kc=kc: A.copy(xb[:, kc, :], xs[s][:]), reads=[("xs", s)], writes=["xb"])

                def epi(mi, pss, pkey, tt=tt):
                    o = osb[mi % 2]
                    T.op("act", lambda: A.copy(o[:, 0:512], pss[0][:, :]), reads=[pkey], writes=[("osb", mi % 2)])
                    T.op("act", lambda: A.copy(o[:, 512:1024], pss[1][:, :]), reads=[pkey], writes=[("osb", mi % 2)])
                    T.dma("pool", projT[mi * 128:(mi + 1) * 128, tt * 1024:(tt + 1) * 1024], o[:],
                          reads=[("osb", mi % 2)], writes=["projT"], slot=f"osb{mi % 2}")

                gemm(T, nc, "g1", w_in, D, [i * 128 for i in range(8)], xb, "xb", [(0, 512), (512, 512)], epi)
        T.barrier()
        if debug:
            for i in range(8):
                T.dma("sp", dbg_proj[i * 128:(i + 1) * 128, :], projT[i * 128:(i + 1) * 128, :], reads=["projT"], slot="dbg")
            T.barrier()
        if stop_after <= 1:
            return nc

        with ExitStack() as ph:
            pp = ExitStack()
            cur = [ph]

            def tl(name, shape, dt=F32):
                return _sb(cur[0], nc, "s_" + name, shape, dt)

            Wb = tl("Wb", [16, 16, 128])
            NR = 13
            pwr = tl("pwr", [128, NR, 16]); pwi = tl("pwi", [128, NR, 16])
            jm = tl("jm", [128, 128]); Wc = tl("Wc", [128, 16, 16]); dcol = tl("dcol", [16, 16])
            pp.__enter__()
            cur[0] = pp

            def ew(e, fn, reads, writes):
                T.op(e, fn, reads=reads, writes=writes)

            def ab_calc(P, Fd, lr_d, li_d, ls_d, pfx):
                lr = tl(pfx + "lr", [P, Fd]); li = tl(pfx + "li", [P, Fd]); ls = tl(pfx + "ls", [P, Fd])
                T.dma("sp", lr[:], lr_d[:, :], writes=[pfx + "lr"], slot=pfx + "l1")
                T.dma("sp", li[:], li_d[:, :], writes=[pfx + "li"], slot=pfx + "l2")
                T.dma("sp", ls[:], ls_d[:, :], writes=[pfx + "ls"], slot=pfx + "l3")
                step = tl(pfx + "step", [P, Fd]); mag = tl(pfx + "mag", [P, Fd]); phs = tl(pfx + "ph", [P, Fd])
                m1 = tl(pfx + "m1", [P, Fd]); sn = tl(pfx + "sn", [P, Fd]); cs = tl(pfx + "cs", [P, Fd])
                abr = tl(pfx + "abr", [P, Fd]); abi = tl(pfx + "abi", [P, Fd])
                k = lambda n: pfx + n
                ew("act", lambda: A.activation(out=step[:], in_=ls[:], func=AF.Exp), [k("ls")], [k("step")])
                ew("dve", lambda: V.tensor_tensor(mag[:], lr[:], step[:], ALU.mult), [k("lr"), k("step")], [k("mag")])
                ew("act", lambda: A.activation(out=mag[:], in_=mag[:], func=AF.Exp), [k("mag")], [k("mag")])
                ew("dve", lambda: V.tensor_tensor(phs[:], li[:], step[:], ALU.mult), [k("li"), k("step")], [k("ph")])
                m2 = tl(pfx + "m2", [P, Fd])

                def wrap(shift):
                    ew("dve", lambda: V.tensor_scalar_add(m1[:], phs[:], shift), [k("ph"), k("sn"), k("cs")], [k("m1")])
                    for _ in range(4):
                        ew("dve", lambda: V.tensor_scalar(m2[:], m1[:], PI, -2 * PI, ALU.is_gt, ALU.mult), [k("m1")], [k("m2")])
                        ew("dve", lambda: V.tensor_tensor(m1[:], m1[:], m2[:], ALU.add), [k("m1"), k("m2")], [k("m1")])
                wrap(0.0)
                ew("act", lambda: A.activation(out=sn[:], in_=m1[:], func=AF.Sin), [k("m1")], [k("sn")])
                wrap(0.5 * PI)
                ew("act", lambda: A.activation(out=cs[:], in_=m1[:], func=AF.Sin), [k("m1")], [k("cs")])
                ew("dve", lambda: V.tensor_tensor(abr[:], mag[:], cs[:], ALU.mult), [k("mag"), k("cs")], [k("abr")])
                ew("dve", lambda: V.tensor_tensor(abi[:], mag[:], sn[:], ALU.mult), [k("mag"), k("sn")], [k("abi")])
                return lr, li, abr, abi

            lr, li, abr, abi = ab_calc(16, 1024, lrA, liA, lsA, "A")
            br = tl("br", [16, 1024]); bi = tl("bi", [16, 1024])
            T.dma("sp", br[:], brA[:, :], writes=["br"], slot="Abr")
            T.dma("sp", bi[:], biA[:, :], writes=["bi"], slot="Abi")
            nr = tl("nr", [16, 1024]); den = tl("den", [16, 1024]); t1 = tl("t1", [16, 1024]); t2 = tl("t2", [16, 1024])
            zr = tl("zr", [16, 1024]); zi = tl("zi", [16, 1024])
            ew("dve", lambda: V.tensor_scalar_add(nr[:], abr[:], -1.0), ["Aabr"], ["nr"])
            ew("dve", lambda: V.tensor_tensor(den[:], lr[:], lr[:], ALU.mult), ["Alr"], ["den"])
            ew("dve", lambda: V.tensor_tensor(t1[:], li[:], li[:], ALU.mult), ["Ali"], ["t1"])
            ew("dve", lambda: V.tensor_tensor(den[:], den[:], t1[:], ALU.add), ["den", "t1"], ["den"])
            ew("dve", lambda: V.reciprocal(den[:], den[:]), ["den"], ["den"])
            ew("dve", lambda: V.tensor_tensor(t1[:], nr[:], lr[:], ALU.mult), ["nr", "Alr", "den"], ["t1"])
            ew("dve", lambda: V.tensor_tensor(t2[:], abi[:], li[:], ALU.mult), ["Aabi", "Ali"], ["t2"])
            ew("dve", lambda: V.tensor_tensor(t1[:], t1[:], t2[:], ALU.add), ["t1", "t2"], ["t1"])
            ew("dve", lambda: V.tensor_tensor(zr[:], t1[:], den[:], ALU.mult), ["t1", "den"], ["zr"])
            ew("dve", lambda: V.tensor_tensor(t1[:], abi[:], lr[:], ALU.mult), ["Aabi", "Alr", "zr"], ["t1"])
            ew("dve", lambda: V.tensor_tensor(t2[:], nr[:], li[:], ALU.mult), ["nr", "Ali", "zr"], ["t2"])
            ew("dve", lambda: V.tensor_tensor(t1[:], t1[:], t2[:], ALU.subtract), ["t1", "t2"], ["t1"])
            ew("dve", lambda: V.tensor_tensor(zi[:], t1[:], den[:], ALU.mult), ["t1", "den"], ["zi"])
            v3 = lambda t: t[:].rearrange("p (g n) -> p g n", g=16)
            ew("dve", lambda: V.tensor_tensor(t1[:], zr[:], br[:], ALU.mult), ["zr", "br", "zi"], ["t1"])
            ew("dve", lambda: V.tensor_tensor(t2[:], zi[:], bi[:], ALU.mult), ["zi", "bi"], ["t2"])
            ew("dve", lambda: V.tensor_tensor(Wb[:, :, 0:64], v3(t1), v3(t2), ALU.subtract), ["t1", "t2"], ["Wb"])
            ew("dve", lambda: V.tensor_tensor(t1[:], zr[:], bi[:], ALU.mult), ["zr", "bi", "Wb"], ["t1"])
            ew("dve", lambda: V.tensor_tensor(t2[:], zi[:], br[:], ALU.mult), ["zi", "br", "Wb"], ["t2"])
            ew("dve", lambda: V.tensor_tensor(Wb[:, :, 64:128], v3(t1), v3(t2), ALU.add), ["t1", "t2"], ["Wb"])
            _, _, bar, bai = ab_calc(128, 16, lrB, liB, lsB, "B")
            q1 = tl("q1", [128, 16]); q2 = tl("q2", [128, 16])
            ew("dve", lambda: V.tensor_copy(pwr[:, 0, :], bar[:]), ["Babr"], ["pw"])
            ew("dve", lambda: V.tensor_copy(pwi[:, 0, :], bai[:]), ["Babi"], ["pw"])
            for r in range(1, NR):
                ew("dve", lambda r=r: V.tensor_tensor(q1[:], pwr[:, r - 1, :], pwr[:, r - 1, :], ALU.mult), ["pw"], ["q1"])
                ew("dve", lambda r=r: V.tensor_tensor(q2[:], pwi[:, r - 1, :], pwi[:, r - 1, :], ALU.mult), ["pw"], ["q2"])
                ew("dve", lambda r=r: V.tensor_tensor(pwr[:, r, :], q1[:], q2[:], ALU.subtract), ["q1", "q2"], ["pw"])
                ew("dve", lambda r=r: V.tensor_tensor(q1[:], pwr[:, r - 1, :], pwi[:, r - 1, :], ALU.mult), ["pw"], ["q1"])
                ew("dve", lambda r=r: V.tensor_scalar_mul(pwi[:, r, :], q1[:], 2.0), ["q1"], ["pw"])
            T.dma("sp", jm[:], jmat_in[:, :], writes=["jm"], slot="cjm")
            T.dma("sp", Wc[:].rearrange("p g h -> p (g h)"), cT[:, :], writes=["Wc"], slot="cwc")
            ew("dve", lambda: V.tensor_scalar_mul(Wc[64:128, :, :], Wc[64:128, :, :], -1.0), ["Wc"], ["Wc"])
            T.dma("sp", dcol[:], dT[:, :], writes=["dcol"], slot="cdc")
            T.barrier()
            pp.__exit__(None, None, None)
            cur[0] = ph

            H = [tl("H0", [128, S]), tl("H1", [128, S])]
            ug = tl("ug", [16, S])
            ysb = tl("ysb", [16, 512]); g1 = tl("g1", [16, 512]); g2 = tl("g2", [16, 512]); yb = tl("yb", [16, S], BF16)
            Rt = [tl("Rt0", [128, 128]), tl("Rt1", [128, 128])]
            pss = [_ps(ph, nc, f"s_ps{i}", [128, 512]) for i in range(4)]
            NJ = S // 512
            pi = 0
            for g in range(16):
                T.dma("sp", ug[:], projT[16 * g:16 * g + 16, :], reads=["projT"], writes=["ug"], slot="ug")
                for j in range(NJ):
                    p = pss[pi % 4]; pk = ("sps", pi % 4); pi += 1
                    T.group("pe", [lambda p=p, j=j, g=g: nc.tensor.matmul(p[:, :], Wb[:, g, :], ug[:, j * 512:(j + 1) * 512], start=True, stop=True)],
                            reads=["Wb", "ug"], writes=[pk])
                    T.op("act", lambda p=p, j=j: A.copy(H[0][:, j * 512:(j + 1) * 512], p[:, :]), reads=[pk], writes=[("H", 0, j)])
                for r in range(NR):
                    d = 1 << r
                    src = H[r % 2]; dst = H[(r + 1) % 2]; sb_ = r % 2; db_ = (r + 1) % 2
                    R = Rt[r % 2]; rk = ("Rt", r % 2)
                    T.op("dve", lambda R=R, r=r, g=g: V.tensor_scalar_mul(R[:], ident[:], pwr[:, r, g:g + 1]), reads=["ident", "pw"], writes=[rk])
                    T.op("dve", lambda R=R, r=r, g=g: V.scalar_tensor_tensor(R[:], jm[:], pwi[:, r, g:g + 1], R[:], ALU.mult, ALU.add),
                         reads=["jm", "pw", rk], writes=[rk])
                    for j in range(NJ):
                        lo = j * 512; hi = lo + 512
                        if hi <= d:
                            T.op("pool", lambda lo=lo, hi=hi, src=src, dst=dst: nc.gpsimd.tensor_copy(dst[:, lo:hi], src[:, lo:hi]),
                                 reads=[("H", sb_, j)], writes=[("H", db_, j)])
                            continue
                        s0 = max(lo, d)
                        rj = sorted(set([(s0 - d) // 512, (hi - d - 1) // 512]))
                        p = pss[pi % 4]; pk = ("sps", pi % 4); pi += 1
                        T.group("pe", [lambda p=p, s0=s0, lo=lo, hi=hi, d=d, R=R, src=src: nc.tensor.matmul(
                            p[:, s0 - lo:512], R[:], src[:, s0 - d:hi - d], start=True, stop=True)],
                            reads=[rk] + [("H", sb_, x) for x in rj], writes=[pk])
                        T.op("dve", lambda p=p, s0=s0, lo=lo, hi=hi, src=src, dst=dst: V.tensor_tensor(
                            dst[:, s0:hi], p[:, s0 - lo:512], src[:, s0:hi], ALU.add),
                            reads=[pk, ("H", sb_, j)], writes=[("H", db_, j)])
                        if s0 > lo:
                            T.op("pool", lambda lo=lo, s0=s0, src=src, dst=dst: nc.gpsimd.tensor_copy(dst[:, lo:s0], src[:, lo:s0]),
                                 reads=[("H", sb_, j)], writes=[("H", db_, j)])
                fb = NR % 2
                for j in range(NJ):
                    p = pss[pi % 4]; pk = ("sps", pi % 4); pi += 1
                    T.group("pe", [lambda p=p, j=j, g=g: nc.tensor.matmul(p[0:16, :], Wc[:, g, :], H[fb][:, j * 512:(j + 1) * 512], start=True, stop=True)],
                            reads=["Wc", ("H", fb, j)], writes=[pk])
                    T.op("dve", lambda p=p, j=j, g=g: V.scalar_tensor_tensor(ysb[:], ug[:, j * 512:(j + 1) * 512],
                                                                            dcol[:, g:g + 1], p[0:16, :], ALU.mult, ALU.add),
                         reads=[pk, "ug", "dcol"], writes=["ysb"])
                    ew("dve", lambda: V.tensor_tensor(g1[:], ysb[:], ysb[:], ALU.mult), ["ysb"], ["g1"])
                    ew("dve", lambda: V.tensor_scalar(g1[:], g1[:], 0.044715, 1.0, ALU.mult, ALU.add), ["g1"], ["g1"])
                    ew("dve", lambda: V.tensor_tensor(g1[:], g1[:], ysb[:], ALU.mult), ["g1", "ysb"], ["g1"])
                    ew("act", lambda: A.activation(out=g2[:], in_=g1[:], func=AF.Sigmoid, scale=1.5957691216057308), ["g1"], ["g2"])
                    ew("dve", lambda j=j: V.tensor_tensor(yb[:, j * 512:(j + 1) * 512], g2[:], ysb[:], ALU.mult), ["g2", "ysb"], ["yb"])
                T.dma("sp", ag_in[16 * g:16 * g + 16, PADC:], yb[:], reads=["yb"], writes=["ag_in"], slot="yb")
        T.barrier()

        if stop_after <= 2:
            if debug:
                for i in range(4):
                    T.dma("sp", dbg_ag[i * 128:(i + 1) * 128, :], ag_in[i * 128:(i + 1) * 128, :], reads=["ag_in"], slot="dbg")
                T.barrier()
            return nc
        with ExitStack() as ph:
            def tl(name, shape, dt=F32):
                return _sb(ph, nc, "a_" + name, shape, dt)
            QT = tl("QT", [128, 2, S], BF16); KT = tl("KT", [128, 2, S], BF16)
            Vt = tl("Vt", [128, 64, 264], BF16)
            stg = [tl(f"stg{i}", [128, 2048]) for i in range(2)]
            vtb = tl("vtb", [128, 2048], BF16)
            tp = [_ps(ph, nc, f"a_tp{i}", [128, 128], BF16) for i in range(2)]
            si = 0
            for ci, dstT in ((0, QT), (1, QT), (2, KT), (3, KT)):
                row0 = 256 + ci * 128
                for c4 in range(S // 2048):
                    s = si % 2; si += 1
                    T.dma("sp", stg[s][:], projT[row0:row0 + 128, c4 * 2048:(c4 + 1) * 2048], reads=["projT"], writes=[("stg", s)], slot=f"stg{s}")
                    T.op("act", lambda s=s, dstT=dstT, ci=ci, c4=c4: A.copy(dstT[:, ci % 2, c4 * 2048:(c4 + 1) * 2048], stg[s][:]),
                         reads=[("stg", s)], writes=["QK"])
            T.op("dve", lambda: V.memset(Vt[:, :, 256:257], 1.0), writes=["Vt"])
            ti = 0
            for half in range(2):
                for c4 in range(S // 2048):
                    s = si % 2; si += 1
                    T.dma("sp", stg[s][:], projT[768 + half * 128:768 + (half + 1) * 128, c4 * 2048:(c4 + 1) * 2048], reads=["projT"], writes=[("stg", s)], slot=f"stg{s}")
                    T.op("act", lambda s=s: A.copy(vtb[:], stg[s][:]), reads=[("stg", s)], writes=["vtb"])
                    for b in range(16):
                        t = tp[ti % 2]; tk = ("tp", ti % 2); ti += 1
                        blk = c4 * 16 + b
                        T.group("pe", [lambda t=t, b=b: nc.tensor.transpose(t[:, :], vtb[:, b * 128:(b + 1) * 128], identb[:])],
                                reads=["vtb", "identb"], writes=[tk])
                        T.op("dve", lambda t=t, blk=blk, half=half: V.tensor_copy(Vt[:, blk, half * 128:(half + 1) * 128], t[:, :]),
                             reads=[tk], writes=["Vt"])
            la = tl("la", [128, 128]); lb = tl("lb", [128, 128]); e1 = tl("e1", [128, 1]); e2 = tl("e2", [128, 1]); nlam = tl("nlam", [128, 1])
            for (qa, ka, eo, nm) in ((lq1, lk1, e1, "e1"), (lq2, lk2, e2, "e2")):
                T.dma("sp", la[:], qa[:, :], writes=["la"], slot="la")
                T.dma("sp", lb[:], ka[:, :], writes=["lb"], slot="la")
                T.op("dve", lambda: V.tensor_tensor(la[:], la[:], lb[:], ALU.mult), reads=["la", "lb"], writes=["la"])
                T.op("dve", lambda eo=eo: V.reduce_sum(eo[:], la[:], axis=AX.X), reads=["la"], writes=[nm])
                T.op("act", lambda eo=eo: A.activation(out=eo[:], in_=eo[:], func=AF.Exp), reads=[nm], writes=[nm])
            T.op("dve", lambda: V.tensor_tensor(nlam[:], e2[:], e1[:], ALU.subtract), reads=["e1", "e2"], writes=["nlam"])
            T.op("dve", lambda: V.tensor_scalar_add(nlam[:], nlam[:], -LAMBDA_INIT), reads=["nlam"], writes=["nlam"])
            gb = tl("gb", [128, 256]); cf = tl("cf", [128, 1]); bn = tl("bn", [128, 3, 512])
            T.dma("sp", gb[:], gsub[:, :], writes=["gb"], slot="gb")
            T.dma("sp", cf[:], cfar[:, :], writes=["cf"], slot="cfs")
            for i in range(3):
                T.dma("sp", bn[:, i, :], bnear[i, :, :], writes=["bn"], slot="bns")
            T.op("dve", lambda: V.tensor_scalar_mul(gb[:], gb[:], 1.0 - LAMBDA_INIT), reads=["gb"], writes=["gb"])

            sps = [_ps(ph, nc, f"a_s{i}", [128, 512]) for i in range(2)]
            acc = [[_ps(ph, nc, f"a_acc{c}{u}", [128, 512]) for u in range(2)] for c in range(2)]
            PT = [tl(f"PT{i}", [128, 512], BF16) for i in range(2)]
            tmpb = tl("tmpb", [128, 512])
            o = tl("o", [128, 256]); ob = tl("ob", [128, 256], BF16); sq = tl("sq", [128, 256])
            r1 = tl("r1", [128, 1]); r2 = tl("r2", [128, 1]); ss = tl("ss", [128, 1])
            yT = [tl(f"yT{i}", [128, 256], BF16) for i in range(2)]
            scale = 128.0 ** -0.5
            ui = 0
            for g in range(S // 256):
                q0 = g * 256
                nkb = 2 * g + 2
                for kb in range(nkb):
                    s = ui % 2; ui += 1
                    sp_ = sps[s]; sk = ("sps", s)
                    T.group("pe", [lambda c=c, sp_=sp_, kb=kb, q0=q0: nc.tensor.matmul(sp_[:, c * 256:(c + 1) * 256], KT[:, c, kb * 128:(kb + 1) * 128],
                                                                                     QT[:, c, q0:q0 + 256], start=True, stop=True) for c in range(2)],
                            reads=["QK"], writes=[sk])
                    near = kb - (2 * g - 1)
                    if near >= 0:
                        T.op("dve", lambda sp_=sp_, near=near: V.scalar_tensor_tensor(tmpb[:], sp_[:, :], scale, bn[:, near, :], ALU.mult, ALU.add),
                             reads=[sk, "bn"], writes=["tmpb"])
                        T.op("act", lambda s=s: A.activation(out=PT[s][:], in_=tmpb[:], func=AF.Exp), reads=["tmpb"], writes=[("PT", s)])
                    else:
                        T.op("act", lambda s=s, sp_=sp_: A.activation(out=PT[s][:], in_=sp_[:, :], func=AF.Exp, bias=cf[:, 0:1], scale=scale),
                             reads=[sk, "cf"], writes=[("PT", s)])
                    T.group("pe", [lambda c=c, u=u, s=s, kb=kb, nkb=nkb: nc.tensor.matmul(acc[c][u][:, 0:257], PT[s][:, c * 256 + u * 128:c * 256 + (u + 1) * 128],
                                                                                       Vt[:, kb, 0:257], start=(kb == 0), stop=(kb == nkb - 1))
                                   for c in range(2) for u in range(2)],
                            reads=[("PT", s), "Vt"], writes=["acc"])
                for u in range(2):
                    a1 = acc[0][u]; a2 = acc[1][u]
                    T.op("dve", lambda a1=a1: V.reciprocal(r1[:], a1[:, 256:257]), reads=["acc"], writes=["r1"])
                    T.op("dve", lambda a2=a2: V.reciprocal(r2[:], a2[:, 256:257]), reads=["acc"], writes=["r2"])
                    T.op("dve", lambda: V.tensor_tensor(r2[:], r2[:], nlam[:], ALU.mult), reads=["r2", "nlam"], writes=["r2"])
                    T.op("dve", lambda a1=a1: V.tensor_scalar_mul(o[:], a1[:, 0:256], r1[:, 0:1]), reads=["acc", "r1"], writes=["o"])
                    T.op("dve", lambda a2=a2: V.scalar_tensor_tensor(o[:], a2[:, 0:256], r2[:, 0:1], o[:], ALU.mult, ALU.add), reads=["acc", "r2", "o"], writes=["o"])
                    T.op("dve", lambda: V.tensor_tensor(sq[:], o[:], o[:], ALU.mult), reads=["o"], writes=["sq"])
                    T.op("dve", lambda: V.reduce_sum(ss[:], sq[:], axis=AX.X), reads=["sq"], writes=["ss"])
                    T.op("dve", lambda: V.tensor_scalar(ss[:], ss[:], 1.0 / 256.0, EPS, ALU.mult, ALU.add), reads=["ss"], writes=["ss"])
                    T.op("act", lambda: A.activation(out=ss[:], in_=ss[:], func=AF.Sqrt), reads=["ss"], writes=["ss"])
                    T.op("dve", lambda: V.reciprocal(ss[:], ss[:]), reads=["ss"], writes=["ss"])
                    T.op("dve", lambda: V.scalar_tensor_tensor(ob[:], o[:], ss[:, 0:1], gb[:], ALU.mult, ALU.mult), reads=["o", "ss", "gb"], writes=["ob"])
                    for hf in range(2):
                        t = tp[ti % 2]; tk = ("tp", ti % 2); ti += 1
                        T.group("pe", [lambda t=t, hf=hf: nc.tensor.transpose(t[:, :], ob[:, hf * 128:(hf + 1) * 128], identb[:])],
                                reads=["ob", "identb"], writes=[tk])
                        T.op("act", lambda t=t, hf=hf, u=u: A.copy(yT[hf][:, u * 128:(u + 1) * 128], t[:, :]), reads=[tk], writes=[("yT", hf)])
                for hf in range(2):
                    T.dma("sp", ag_in[256 + hf * 128:256 + (hf + 1) * 128, PADC + q0:PADC + q0 + 256], yT[hf][:],
                          reads=[("yT", hf)], writes=["ag_in"], slot=f"yT{hf}")
        T.barrier()
        if debug:
            for i in range(4):
                T.dma("sp", dbg_ag[i * 128:(i + 1) * 128, :], ag_in[i * 128:(i + 1) * 128, :], reads=["ag_in"], slot="dbg")
            T.barrier()

        if phase_c:
            T._deps("pool", ["ag_in"], ["ag_out"])
            ccs = T._mk("CC")
            nc.gpsimd.collective_compute("AllGather", ALU.bypass, replica_groups=[list(range(NCORE))],
                                         ins=[ag_in.ap().opt()], outs=[ag_out.ap().opt()]).then_inc(T.sem[ccs], 1)
            T.cnt[ccs] += 1
            T._record((ccs, T.cnt[ccs]), ["ag_in"], ["ag_out"])
            T.barrier()
            build_phase_c(nc, T, dr, ag_out, agsel, h1s, p2s, outT, ones, pc_level, dbgc)
        T.barrier()
    return nc


def build_phase_c(nc, T, dr, ag_out, agsel, h1s, p2s, outT, ones, pc_level=3, dbgc=None, sub=9):
    V = nc.vector
    A = nc.scalar
    with ExitStack() as pc:
        def cl(name, shape, dt=F32):
            return _sb(pc, nc, "c_" + name, shape, dt)
        bglu = cl("bglu", [128, 16]); l1g = cl("l1g", [128, 32]); l1b = cl("l1b", [128, 32])
        l2g = cl("l2g", [128, 32]); l2b = cl("l2b", [128, 32]); cwt = cl("cwt", [128, 3 * 86]); cbt = cl("cbt", [128, 86]); hmt = cl("hmt", [128, NTILE])
        for t_, d_ in ((bglu, "b_glu"), (l1g, "ln1g"), (l1b, "ln1b"), (l2g, "ln2g"), (l2b, "ln2b"), (cwt, "cw"), (cbt, "cb"), (hmt, "hm")):
            T.dma("sp", t_[:], dr[d_][:, :], writes=["cparams"], slot="cpar")
        R1 = cl("R1", [128, 32, NTP], BF16)
        oht = cl("oht", [128, NCORE])
        T.dma("sp", oht[:], dr["oh"][:, :], writes=["oht"], slot="ohs")
        with ExitStack() as ph:
            sacc = _sb(ph, nc, "sel_acc", [128, 32, 1024 + PADC], BF16)
            win = [_sb(ph, nc, f"sel_w{i}", [128, 1024 + PADC], BF16) for i in range(2)]
            T.op("dve", lambda: V.memset(sacc[:], 0.0), writes=["sacc"])
            it = 0
            for r in range(NCORE):
                for ch in range(32):
                    s = it % 2; it += 1
                    T.dma("pool", win[s][:], ag_out[ch * 128:(ch + 1) * 128, 1024 * r:1024 * r + 1024 + PADC],
                          reads=["ag_out"], writes=[("win", s)], slot=f"win{s}")
                    T.op("dve", lambda s=s, ch=ch, r=r: V.scalar_tensor_tensor(sacc[:, ch, :], win[s][:], oht[:, r:r + 1], sacc[:, ch, :], ALU.mult, ALU.add),
                         reads=[("win", s), "oht", "sacc"], writes=["sacc"])
            for ch in range(32):
                T.dma("sp", agsel[ch * 128:(ch + 1) * 128, :], sacc[:, ch, :], reads=["sacc"], writes=["agsel"], slot="selout")
            T.barrier()
            if pc_level == 1:
                for ch in range(32):
                    T.dma("sp", dbgc[ch * 128:(ch + 1) * 128, :], agsel[ch * 128:(ch + 1) * 128, :], slot="dbgc")
                T.barrier()
                return
        for tt in range(NTILE):
            with ExitStack() as ph:
                ygel = _sb(ph, nc, "c_ygel", [128, 16, NTP], BF16)
                pre = _sb(ph, nc, "c_pre", [128, 32, NT], F32)
                s1 = _sb(ph, nc, "c_s1", [128, NT], F32); s2 = _sb(ph, nc, "c_s2", [128, NT], F32)
                tq = _sb(ph, nc, "c_tq", [128, NT], F32)
                xr = [_sb(ph, nc, f"c_xr{i}", [128, NT], F32) for i in range(2)]
                sgt = _sb(ph, nc, "c_sg", [128, NT], F32)
                c0 = tt * TT
                for kc in range(16):
                    r = kc // 2
                    T.dma("pool", ygel[:, kc, 0:NT], agsel[512 * r + (kc % 2) * 128:512 * r + (kc % 2) * 128 + 128, c0:c0 + NT],
                          reads=["agsel"], writes=["ygel"], slot="ygld")
                    T.dma("pool", R1[:, 16 + kc, 0:NT], agsel[512 * r + 256 + (kc % 2) * 128:512 * r + 256 + (kc % 2) * 128 + 128, c0:c0 + NT],
                          reads=["agsel"], writes=["R1"], slot="r1ld")
                tok = [(0, 264), (264, 250)]
                if sub <= 1:
                    T.barrier()
                    return

                def epi_glu(mi, pss, pkey):
                    for si, (t0, ln) in enumerate(tok):
                        T.op("act", lambda si=si, t0=t0, ln=ln: A.activation(out=sgt[:, t0:t0 + ln], in_=pss[si][:, 0:ln], func=AF.Sigmoid, bias=bglu[:, mi:mi + 1]),
                             reads=[pkey, "cparams"], writes=["sgt"])
                    T.op("dve", lambda: V.tensor_tensor(R1[:, mi, 0:NT], sgt[:], ygel[:, mi, 0:NT], ALU.mult), reads=["sgt", "ygel"], writes=["R1"])
                gemm(T, nc, "gG", dr["w_glu"], 2048, [i * 128 for i in range(16)], ygel, "ygel", tok, epi_glu)
                if sub <= 2:
                    T.barrier()
                    return
                T.op("dve", lambda: V.memset(s1[:], 0.0), writes=["s1"])
                T.op("dve", lambda: V.memset(s2[:], 0.0), writes=["s2"])

                def epi_out(mi, pss, pkey):
                    x_ = xr[mi % 2]
                    T.dma("pool", x_[:], dr["xTc"][mi * 128:(mi + 1) * 128, c0:c0 + NT], writes=[("xr", mi % 2)], slot=f"xr{mi % 2}")
                    for si, (t0, ln) in enumerate(tok):
                        T.op("dve", lambda si=si, t0=t0, ln=ln: V.scalar_tensor_tensor(pre[:, mi, t0:t0 + ln], x_[:, t0:t0 + ln], ALPHA, pss[si][:, 0:ln], ALU.mult, ALU.add),
                             reads=[pkey, ("xr", mi % 2)], writes=["pre"])
                    T.op("pool", lambda: nc.gpsimd.tensor_tensor(s1[:], s1[:], pre[:, mi, :], ALU.add), reads=["pre", "s1"], writes=["s1"])
                    T.op("act", lambda: A.activation(out=tq[:], in_=pre[:, mi, :], func=AF.Square), reads=["pre"], writes=["tq"])
                    T.op("pool", lambda: nc.gpsimd.tensor_tensor(s2[:], s2[:], tq[:], ALU.add), reads=["tq", "s2"], writes=["s2"])
                gemm(T, nc, "gO", dr["w_out"], D, [i * 128 for i in range(32)], R1, "R1", tok, epi_out)
                T.barrier()
                if sub <= 3:
                    return
                ln_finish(nc, T, ph, s1, s2, tok, ones, "l1")
                mean_b, rstd_b = s1, s2
                if sub <= 4:
                    T.barrier()
                    return
                for mi in range(32):
                    T.op("dve", lambda mi=mi: V.tensor_tensor(pre[:, mi, :], pre[:, mi, :], mean_b[:], ALU.subtract), reads=["pre", "s1"], writes=["pre"])
                    T.op("dve", lambda mi=mi: V.tensor_tensor(pre[:, mi, :], pre[:, mi, :], rstd_b[:], ALU.mult), reads=["pre", "s2"], writes=["pre"])
                    T.op("dve", lambda mi=mi: V.tensor_scalar(pre[:, mi, :], pre[:, mi, :], l1g[:, mi:mi + 1], l1b[:, mi:mi + 1], ALU.mult, ALU.add),
                         reads=["pre", "cparams"], writes=["pre"])
                    T.op("act", lambda mi=mi: A.copy(R1[:, mi, 0:NT], pre[:, mi, :]), reads=["pre"], writes=["R1"])
                    T.op("dve", lambda mi=mi: V.tensor_scalar_mul(R1[:, mi, 0:PADC], R1[:, mi, 0:PADC], hmt[:, tt:tt + 1]), reads=["R1", "cparams"], writes=["R1"])
                    T.dma("sp", h1s[mi * 128:(mi + 1) * 128, :], pre[:, mi, PADC:], reads=["pre"], writes=["h1s"], slot="h1s")
                T.barrier()
                if pc_level == 2:
                    for ch in range(32):
                        T.dma("sp", dbgc[ch * 128:(ch + 1) * 128, :], h1s[ch * 128:(ch + 1) * 128, :], slot="dbgc")
                    T.barrier()
                    return
            with ExitStack() as ph:
                hff = _sb(ph, nc, "c_hff", [128, 86, TT], BF16)
                asb = _sb(ph, nc, "c_asb", [128, TT], F32)
                gs = _sb(ph, nc, "c_gs", [128, NT], F32)
                t1 = _sb(ph, nc, "c_t1", [128, TT], F32)
                sg = _sb(ph, nc, "c_sgl", [128, TT], F32)
                tok = [(0, 264), (264, 250)]
                cols = []
                for f in range(86):
                    cols += [f * 128, DFF + f * 128]

                def epi_up(mi, pss, pkey):
                    f = mi // 2
                    if mi % 2 == 0:
                        T.op("act", lambda: A.copy(asb[:, 0:262], pss[0][:, PADC:264]), reads=[pkey], writes=["asb"])
                        T.op("act", lambda: A.copy(asb[:, 262:TT], pss[1][:, 0:250]), reads=[pkey], writes=["asb"])
                    else:
                        T.op("act", lambda: A.copy(gs[:, 0:264], pss[0][:, 0:264]), reads=[pkey], writes=["gs"])
                        T.op("act", lambda: A.copy(gs[:, 264:NT], pss[1][:, 0:250]), reads=[pkey], writes=["gs"])
                        T.op("dve", lambda: V.tensor_scalar(t1[:], gs[:, 2:NT], cwt[:, 2 * 86 + f:2 * 86 + f + 1], cbt[:, f:f + 1], ALU.mult, ALU.add),
                             reads=["gs", "cparams"], writes=["t1"])
                        T.op("dve", lambda: V.scalar_tensor_tensor(t1[:], gs[:, 1:NT - 1], cwt[:, 86 + f:86 + f + 1], t1[:], ALU.mult, ALU.add),
                             reads=["gs", "t1"], writes=["t1"])
                        T.op("dve", lambda: V.scalar_tensor_tensor(t1[:], gs[:, 0:NT - 2], cwt[:, f:f + 1], t1[:], ALU.mult, ALU.add),
                             reads=["gs", "t1"], writes=["t1"])
                        T.op("act", lambda: A.activation(out=sg[:], in_=t1[:], func=AF.Silu), reads=["t1"], writes=["sg"])
                        T.op("dve", lambda: V.tensor_tensor(hff[:, f, :], sg[:], asb[:], ALU.mult), reads=["sg", "asb"], writes=["hff"])
                gemm(T, nc, "gU", dr["w_up"], D, cols, R1, "R1", tok, epi_up)
                T.barrier()
                pr = [_sb(ph, nc, f"c_pr{i}", [128, TT], F32) for i in range(2)]
                s1 = _sb(ph, nc, "c_s1b", [128, TT], F32); s2 = _sb(ph, nc, "c_s2b", [128, TT], F32)
                tq = _sb(ph, nc, "c_tqb", [128, TT], F32)
                hr = [_sb(ph, nc, f"c_hr{i}", [128, TT], F32) for i in range(2)]
                tok2 = [(0, 256), (256, 256)]
                T.op("dve", lambda: V.memset(s1[:], 0.0), writes=["s1"])
                T.op("dve", lambda: V.memset(s2[:], 0.0), writes=["s2"])

                def epi_dn(mi, pss, pkey):
                    h_ = hr[mi % 2]; p_ = pr[mi % 2]; pk = ("pr", mi % 2)
                    T.dma("pool", h_[:], h1s[mi * 128:(mi + 1) * 128, :], reads=["h1s"], writes=[("hr", mi % 2)], slot=f"hr{mi % 2}")
                    for si, (t0, ln) in enumerate(tok2):
                        T.op("dve", lambda si=si, t0=t0, ln=ln: V.scalar_tensor_tensor(p_[:, t0:t0 + ln], h_[:, t0:t0 + ln], ALPHA, pss[si][:, 0:ln], ALU.mult, ALU.add),
                             reads=[pkey, ("hr", mi % 2)], writes=[pk])
                    T.op("pool", lambda: nc.gpsimd.tensor_tensor(s1[:], s1[:], p_[:], ALU.add), reads=[pk, "s1"], writes=["s1"])
                    T.op("act", lambda: A.activation(out=tq[:], in_=p_[:], func=AF.Square), reads=[pk], writes=["tq"])
                    T.op("pool", lambda: nc.gpsimd.tensor_tensor(s2[:], s2[:], tq[:], ALU.add), reads=["tq", "s2"], writes=["s2"])
                    T.dma("sp", p2s[mi * 128:(mi + 1) * 128, :], p_[:], reads=[pk], writes=["p2s"], slot=f"prs{mi % 2}")
                gemm(T, nc, "gD", dr["w_dn"], DFF, [i * 128 for i in range(32)], hff, "hff", tok2, epi_dn)
                T.barrier()
                ln_finish(nc, T, ph, s1, s2, tok2, ones, "l2")
                for mi in range(32):
                    p_ = pr[mi % 2]; pk = ("pr", mi % 2)
                    T.dma("pool", p_[:], p2s[mi * 128:(mi + 1) * 128, :], reads=["p2s"], writes=[pk], slot=f"prl{mi % 2}")
                    T.op("dve", lambda p_=p_: V.tensor_tensor(p_[:], p_[:], s1[:], ALU.subtract), reads=[pk, "s1"], writes=[pk])
                    T.op("dve", lambda p_=p_: V.tensor_tensor(p_[:], p_[:], s2[:], ALU.mult), reads=[pk, "s2"], writes=[pk])
                    T.op("dve", lambda p_=p_, mi=mi: V.tensor_scalar(p_[:], p_[:], l2g[:, mi:mi + 1], l2b[:, mi:mi + 1], ALU.mult, ALU.add),
                         reads=[pk, "cparams"], writes=[pk])
                    T.dma("sp", outT[mi * 128:(mi + 1) * 128, tt * TT:(tt + 1) * TT], p_[:], reads=[pk], writes=["outT"], slot=f"outs{mi % 2}")
                T.barrier()


def ln_finish(nc, T, ph, s1, s2, tok, ones, tag):
    V = nc.vector
    pa = [_ps(ph, nc, f"{tag}_pa{i}", [128, 512]) for i in range(2)]
    pb = [_ps(ph, nc, f"{tag}_pb{i}", [128, 512]) for i in range(2)]
    T.group("pe", [lambda si=si, t0=t0, ln=ln: nc.tensor.matmul(pa[si][:, 0:ln], ones[:], s1[:, t0:t0 + ln], start=True, stop=True)
                   for si, (t0, ln) in enumerate(tok)], reads=["ones", "s1"], writes=[tag + "pa"])
    T.group("pe", [lambda si=si, t0=t0, ln=ln: nc.tensor.matmul(pb[si][:, 0:ln], ones[:], s2[:, t0:t0 + ln], start=True, stop=True)
                   for si, (t0, ln) in enumerate(tok)], reads=["ones", "s2"], writes=[tag + "pb"])
    for si, (t0, ln) in enumerate(tok):
        T.op("dve", lambda si=si, t0=t0, ln=ln: V.tensor_scalar_mul(s1[:, t0:t0 + ln], pa[si][:, 0:ln], 1.0 / D), reads=[tag + "pa"], writes=["s1"])
        T.op("dve", lambda si=si, t0=t0, ln=ln: V.tensor_scalar_mul(s2[:, t0:t0 + ln], pb[si][:, 0:ln], 1.0 / D), reads=[tag + "pb"], writes=["s2"])
    tmp = _sb(ph, nc, f"{tag}_tmp", list(s1.shape), F32)
    T.op("dve", lambda: V.tensor_tensor(tmp[:], s1[:], s1[:], ALU.mult), reads=["s1"], writes=[tag + "tmp"])
    T.op("dve", lambda: V.tensor_tensor(s2[:], s2[:], tmp[:], ALU.subtract), reads=["s2", tag + "tmp"], writes=["s2"])
    T.op("dve", lambda: V.tensor_scalar_add(s2[:], s2[:], EPS), reads=["s2"], writes=["s2"])
    T.op("act", lambda: nc.scalar.activation(out=s2[:], in_=s2[:], func=AF.Sqrt), reads=["s2"], writes=["s2"])
    T.op("dve", lambda: V.reciprocal(s2[:], s2[:]), reads=["s2"], writes=["s2"])


def t5_bucket_np(rel):
    half = 16
    max_exact = 8
    ret = np.where(rel > 0, half, 0)
    n = np.abs(rel)
    nf = np.maximum(n, 1).astype(np.float32)
    large = max_exact + (np.log(nf / max_exact) / math.log(128 / max_exact) * (half - max_exact)).astype(np.int32)
    large = np.minimum(large, half - 1)
    return ret + np.where(n < max_exact, n, large)


def host_inputs(inp, phase_c=True):
    f = np.float32
    x = np.asarray(inp["x"], f)[0]
    xT = np.zeros((D, S + PADC), f)
    xT[:, PADC:] = x.T
    w_in = np.asarray(inp["w_in"], f)[0]
    ident = np.eye(128, dtype=f)
    jm = np.zeros((128, 128), f)
    for n in range(64):
        jm[n, 64 + n] = 1.0
        jm[64 + n, n] = -1.0
    lam_re = np.asarray(inp["ssm_lambda_re"], f)[0]; lam_im = np.asarray(inp["ssm_lambda_im"], f)[0]
    ls = np.asarray(inp["ssm_log_step"], f)[0]
    b_re = np.asarray(inp["ssm_b_re"], f)[0]; b_im = np.asarray(inp["ssm_b_im"], f)[0]
    c_re = np.asarray(inp["ssm_c_re"], f)[0]; c_im = np.asarray(inp["ssm_c_im"], f)[0]
    sd = np.asarray(inp["ssm_d"], f)[0]
    rel_bias = np.asarray(inp["rel_bias"], f)
    maps = []
    kk = np.arange(128)[:, None]
    qq = np.arange(256)[None, :]
    for c in range(NCORE):
        m = {}
        m["xT"] = xT
        cols = np.concatenate([np.arange(256 * c, 256 * c + 256), 2048 + np.arange(256 * c, 256 * c + 256),
                               4096 + np.arange(256 * c, 256 * c + 256), 6144 + np.arange(256 * c, 256 * c + 256)])
        m["w_in"] = np.ascontiguousarray(w_in[:, cols])
        m["ident"] = ident; m["jmat"] = jm
        gs = slice(16 * c, 16 * c + 16)
        m["lrA"] = np.ascontiguousarray(np.broadcast_to(lam_re[gs].reshape(1, 1024), (16, 1024)))
        m["liA"] = np.ascontiguousarray(np.broadcast_to(lam_im[gs].reshape(1, 1024), (16, 1024)))
        m["lsA"] = np.ascontiguousarray(np.broadcast_to(np.repeat(ls[gs], 64).reshape(1, 1024), (16, 1024)))
        m["brA"] = np.ascontiguousarray(b_re[gs].transpose(2, 0, 1).reshape(16, 1024))
        m["biA"] = np.ascontiguousarray(b_im[gs].transpose(2, 0, 1).reshape(16, 1024))
        m["lrB"] = np.ascontiguousarray(np.concatenate([lam_re[gs].T, lam_re[gs].T], 0))
        m["liB"] = np.ascontiguousarray(np.concatenate([lam_im[gs].T, lam_im[gs].T], 0))
        m["lsB"] = np.ascontiguousarray(np.broadcast_to(ls[gs].reshape(1, 16), (128, 16)))
        m["cT"] = np.ascontiguousarray(np.concatenate([c_re[gs].transpose(2, 0, 1), c_im[gs].transpose(2, 0, 1)], 0).reshape(128, 256))
        m["dT"] = np.ascontiguousarray(sd[256 * c:256 * c + 256].reshape(16, 16).T)
        for nm in ("lq1", "lk1", "lq2", "lk2"):
            key = {"lq1": "att_lambda_q1", "lk1": "att_lambda_k1", "lq2": "att_lambda_q2", "lk2": "att_lambda_k2"}[nm]
            m[nm] = np.ascontiguousarray(np.broadcast_to(np.asarray(inp[key], f)[0].reshape(1, 128), (128, 128)))
        m["gsub"] = np.ascontiguousarray(np.broadcast_to(np.asarray(inp["att_subln_g"], f)[0].reshape(1, 256), (128, 256)))
        m["cfar"] = np.full((128, 1), rel_bias[15, c], f)
        bn = np.zeros((3, 128, 512), f)
        for i in range(3):
            rel = (128 * (i - 1) + kk) - qq
            b = rel_bias[t5_bucket_np(rel), c].astype(f)
            kpos = 128 * (i - 1) + kk
            vis = (kpos // 64) <= (qq // 64)
            b = np.where(vis, b, f(-1e30)).astype(f)
            bn[i, :, 0:256] = b
            bn[i, :, 256:512] = b
        m["bnear"] = bn
        if phase_c:
            m["w_glu"] = np.asarray(inp["ssm_w_glu"], f)[0]
            m["b_glu"] = np.ascontiguousarray(np.asarray(inp["ssm_b_glu"], f)[0].reshape(16, 128).T)
            m["w_out"] = np.asarray(inp["w_out"], f)[0]
            m["ln1g"] = np.ascontiguousarray(np.asarray(inp["ln1_g"], f)[0].reshape(32, 128).T)
            m["ln1b"] = np.ascontiguousarray(np.asarray(inp["ln1_b"], f)[0].reshape(32, 128).T)
            m["w_up"] = np.asarray(inp["ffn_w_up"], f)[0]
            cwv = np.asarray(inp["ffn_conv_w"], f)[0]
            m["cw"] = np.ascontiguousarray(cwv.reshape(3, 86, 128).transpose(2, 0, 1).reshape(128, 258))
            m["cb"] = np.ascontiguousarray(np.asarray(inp["ffn_conv_b"], f)[0].reshape(86, 128).T)
            m["w_dn"] = np.asarray(inp["ffn_w_down"], f)[0]
            m["ln2g"] = np.ascontiguousarray(np.asarray(inp["ln2_g"], f)[0].reshape(32, 128).T)
            m["ln2b"] = np.ascontiguousarray(np.asarray(inp["ln2_b"], f)[0].reshape(32, 128).T)
            hmv = np.ones((128, NTILE), f)
            if c == 0:
                hmv[:, 0] = 0.0
            m["hm"] = hmv
            ohv = np.zeros((128, NCORE), f); ohv[:, c] = 1.0
            m["oh"] = ohv
            m["xTc"] = np.ascontiguousarray(xT[:, 1024 * c:1024 * c + 1024 + PADC])
        maps.append(m)
    return maps


def kernel(**inputs):
    nc = build(debug=False, phase_c=True)
    maps = host_inputs(inputs, True)
    res = run_bass_kernel_spmd(nc, maps, core_ids=list(range(NCORE)))
    out = np.concatenate([np.asarray(res.results[c]["outT"], np.float32).T for c in range(NCORE)], axis=0)
    return out.reshape(1, S, D)
```

```python
import math
from contextlib import ExitStack
import numpy as np
import concourse.bass as bass
import concourse.mybir as mybir
from concourse.bass_utils import run_bass_kernel_spmd

F32 = mybir.dt.float32
BF16 = mybir.dt.bfloat16
AF = mybir.ActivationFunctionType
ALU = mybir.AluOpType
AX = mybir.AxisListType

NCORE = 8
S = 8192
D = 4096
DFF = 11008
PADC = 2
ALPHA = 2.0 ** 0.25
EPS = 1e-5
LAMBDA_INIT = 0.8 - 0.6 * math.exp(0.0)
TT = 512
NT = TT + 2
NTP = 520
NTILE = 1024 // TT
PI = math.pi


DECLARED = []


class Tracker:
    def __init__(self, nc, es):
        self.nc = nc
        self.es = es
        self.eng = {"pe": nc.tensor, "act": nc.scalar, "dve": nc.vector, "pool": nc.gpsimd, "sp": nc.sync}
        self.sem = {}
        self.cnt = {}
        self.waited = {}
        self.writers = {}
        self.readers = {}
        for e in ("pe", "act", "dve", "pool"):
            self._mk("E_" + e)

    def _mk(self, name):
        if name not in self.sem:
            self.sem[name] = self.es.enter_context(self.nc.semaphore(name))
            self.cnt[name] = 0
        return name

    def _wait(self, e, tok):
        sem, cnt = tok
        if self.waited.get((e, sem), 0) >= cnt:
            return
        self.eng[e].wait_ge(self.sem[sem], cnt)
        self.waited[(e, sem)] = cnt

    def _deps(self, e, reads, writes):
        own = "E_" + e
        for r in reads:
            for sem, cnt in self.writers.get(r, {}).items():
                self._wait(e, (sem, cnt))
        for w in writes:
            for sem, cnt in self.writers.get(w, {}).items():
                if sem == own:
                    continue
                self._wait(e, (sem, cnt))
            for sem, cnt in self.readers.get(w, {}).items():
                if sem == own:
                    continue
                self._wait(e, (sem, cnt))

    def _record(self, tok, reads, writes):
        sem, cnt = tok
        for r in reads:
            self.readers.setdefault(r, {})[sem] = cnt
        for w in writes:
            self.writers.setdefault(w, {})[sem] = cnt

    def op(self, e, fn, reads=(), writes=()):
        self._deps(e, reads, writes)
        ins = fn()
        sem = "E_" + e
        self.cnt[sem] += 1
        ins.then_inc(self.sem[sem], 1)
        self._record((sem, self.cnt[sem]), reads, writes)

    def group(self, e, fns, reads=(), writes=()):
        self._deps(e, reads, writes)
        ins = None
        for f in fns:
            ins = f()
        sem = "E_" + e
        self.cnt[sem] += 1
        ins.then_inc(self.sem[sem], 1)
        self._record((sem, self.cnt[sem]), reads, writes)

    def dma(self, q, out, in_, reads=(), writes=(), slot=None):
        q = "sp"
        self._deps(q, reads, writes)
        sem = self._mk("D_" + str(slot))
        ins = self.eng[q].dma_start(out=out, in_=in_)
        self.cnt[sem] += 16
        ins.then_inc(self.sem[sem], 16)
        self._record((sem, self.cnt[sem]), reads, writes)

    def barrier(self):
        for e in self.eng:
            for sem, cnt in self.cnt.items():
                if cnt > 0:
                    self._wait(e, (sem, cnt))
        self.writers.clear()
        self.readers.clear()


_UID = [0]


def _sb(ph, nc, name, shape, dt):
    _UID[0] += 1
    return ph.enter_context(nc.sbuf_tensor(f"sb_{name}_{_UID[0]}", list(shape), dt))


def _ps(ph, nc, name, shape, dt=F32):
    _UID[0] += 1
    return ph.enter_context(nc.psum_tensor(f"pp_{name}_{_UID[0]}", list(shape), dt))


def gemm(T, nc, tag, Wt, K, npairs, act, actkey, tok, epi, kg=16, pre=None):
    KC = K // 128
    ngr = (KC + kg - 1) // kg
    with ExitStack() as ph:
        st = [_sb(ph, nc, f"{tag}_st{i}", [128, kg, 256], F32) for i in range(2)]
        wb = [_sb(ph, nc, f"{tag}_wb{i}", [128, kg, 256], BF16) for i in range(2)]
        ps = [[[_ps(ph, nc, f"{tag}_ps{i}_{m}_{j}", [128, 512]) for j in range(len(tok))] for m in range(2)] for i in range(2)]
        it = 0
        for mp in range(npairs):
            if pre is not None:
                pre(mp)
            a_t = act(mp) if callable(act) else act
            a_k = actkey(mp) if callable(actkey) else actkey
            pset = ps[mp % 2]
            for gi in range(ngr):
                k0 = gi * kg
                n = min(kg, KC - k0)
                s = it % 2
                it += 1
                src = Wt[mp * 128:(mp + 1) * 128, k0 * 256:(k0 + n) * 256].rearrange("p (k c) -> p k c", c=256)
                T.dma("sp", st[s][:, 0:n, :], src, writes=[(tag, "st", s)], slot=f"{tag}st{s}")
                ce = "dve" if (it % 2 == 0) else "pool"
                T.op(ce, lambda s=s, n=n, ce=ce: T.eng[ce].tensor_copy(wb[s][:, 0:n, :], st[s][:, 0:n, :]),
                     reads=[(tag, "st", s)], writes=[(tag, "wb", s)])
                fns = []
                for j in range(n):
                    for mm in range(2):
                        for si, (t0, ln) in enumerate(tok):
                            fns.append(lambda j=j, mm=mm, si=si, t0=t0, ln=ln, s=s, k0=k0: nc.tensor.matmul(
                                pset[mm][si][:, 0:ln], wb[s][:, j, mm * 128:(mm + 1) * 128], a_t[:, k0 + j, t0:t0 + ln],
                                start=(k0 + j == 0), stop=(k0 + j == KC - 1)))
                T.group("pe", fns, reads=[(tag, "wb", s), a_k], writes=[(tag, "ps", mp % 2)])
            for mm in range(2):
                epi(2 * mp + mm, pset[mm], (tag, "ps", mp % 2))
        T.barrier()


def build(debug=False, phase_c=True, stop_after=9, pc_level=3, skip_a=False, sub=9):
    nc = bass.Bass("TRN2", target_bir_lowering=False)
    dr = {}
    DECLARED.clear()

    def din(name, shape, dt=F32):
        dr[name] = nc.dram_tensor(name, list(shape), dt, kind="ExternalInput")
        DECLARED.append(name)
        return dr[name]

    xT = din("xT", [D, S + PADC])
    w_in = din("w_in", [4 * 128, 32 * 256])
    ident_in = din("ident", [128, 128])
    jmat_in = din("jmat", [128, 128])
    lrA = din("lrA", [16, 1024]); liA = din("liA", [16, 1024]); lsA = din("lsA", [16, 1024])
    brA = din("brA", [16, 1024]); biA = din("biA", [16, 1024])
    lrB = din("lrB", [128, 16]); liB = din("liB", [128, 16]); lsB = din("lsB", [128, 16])
    cT = din("cT", [128, 256])
    dT = din("dT", [16, 16])
    lq1 = din("lq1", [128, 128]); lk1 = din("lk1", [128, 128]); lq2 = din("lq2", [128, 128]); lk2 = din("lk2", [128, 128])
    gsub = din("gsub", [128, 256])
    cfar = din("cfar", [128, 1])
    bnear = din("bnear", [3, 128, 512])
    if phase_c:
        w_glu = din("w_glu", [8 * 128, 16 * 256]); b_glu = din("b_glu", [128, 16])
        w_out = din("w_out", [16 * 128, 32 * 256])
        ln1g = din("ln1g", [128, 32]); ln1b = din("ln1b", [128, 32])
        if pc_level >= 3:
            w_up = din("w_up", [86 * 128, 32 * 256])
            w_dn = din("w_dn", [16 * 128, 86 * 256])
        cw = din("cw", [128, 3 * 86]); cb = din("cb", [128, 86])
        ln2g = din("ln2g", [128, 32]); ln2b = din("ln2b", [128, 32])
        hm = din("hm", [128, NTILE])
        oh = din("oh", [128, NCORE])
        xTc = din("xTc", [D, 1024 + PADC])
        outT = nc.dram_tensor("outT", [D, 1024], F32, kind="ExternalOutput") if pc_level >= 3 else None
    projT = nc.dram_tensor("projT", [1024, S], F32)
    ag_in = nc.dram_tensor("ag_in", [512, S + PADC], BF16)
    ag_out = nc.dram_tensor("ag_out", [NCORE * 512, S + PADC], BF16)
    if phase_c:
        agsel = nc.dram_tensor("agsel", [NCORE * 512, 1024 + PADC], BF16)
        h1s = nc.dram_tensor("h1s", [D, TT], F32)
        dbgc = None
        if pc_level == 1:
            dbgc = nc.dram_tensor("dbg_sel", [NCORE * 512, 1024 + PADC], BF16, kind="ExternalOutput")
        if pc_level == 2:
            dbgc = nc.dram_tensor("dbg_h1", [D, TT], F32, kind="ExternalOutput")
        p2s = nc.dram_tensor("p2s", [D, TT], F32)
    if debug:
        dbg_proj = nc.dram_tensor("dbg_proj", [1024, S], F32, kind="ExternalOutput")
        dbg_ag = nc.dram_tensor("dbg_ag", [512, S + PADC], BF16, kind="ExternalOutput")

    with ExitStack() as es:
        es.enter_context(nc.Block())
        T = Tracker(nc, es)
        V = nc.vector
        A = nc.scalar
        cst = ExitStack()
        es.enter_context(cst)
        ident = _sb(cst, nc, "ident", [128, 128], F32)
        identb = _sb(cst, nc, "identb", [128, 128], BF16)
        ones = _sb(cst, nc, "ones", [128, 128], F32)
        negpi = _sb(cst, nc, "negpi", [128, 1], F32)
        zero = _sb(cst, nc, "zero", [128, 8], BF16)
        T.dma("sp", ident[:], ident_in[:, :], writes=["ident"], slot="c0")
        T.op("dve", lambda: V.tensor_copy(identb[:], ident[:]), reads=["ident"], writes=["identb"])
        T.op("dve", lambda: V.memset(ones[:], 1.0), writes=["ones"])
        T.op("dve", lambda: V.memset(negpi[:], -PI), writes=["negpi"])
        T.op("dve", lambda: V.memset(zero[:], 0.0), writes=["zero"])
        T.dma("sp", ag_in[0:128, 0:PADC], zero[:, 0:PADC], reads=["zero"], slot="z0")
        T.dma("sp", ag_in[128:256, 0:PADC], zero[:, 0:PADC], reads=["zero"], slot="z0")
        T.dma("sp", ag_in[256:384, 0:PADC], zero[:, 0:PADC], reads=["zero"], slot="z0")
        T.dma("sp", ag_in[384:512, 0:PADC], zero[:, 0:PADC], reads=["zero"], slot="z0")

        if skip_a:
            T.barrier()
            build_phase_c(nc, T, dr, ag_out, agsel, h1s, p2s, outT, ones, pc_level, dbgc, sub)
            T.barrier()
            return nc
        with ExitStack() as ph:
            xbs = [_sb(ph, nc, f"p1_xb{i}", [128, 32, 1024], BF16) for i in range(2)]
            xs = [_sb(ph, nc, f"p1_xs{i}", [128, 1024], F32) for i in range(2)]
            osb = [_sb(ph, nc, f"p1_o{i}", [128, 1024], F32) for i in range(2)]
            NTT = S // 1024

            def load_x(tt):
                for kc in range(32):
                    s = kc % 2
                    T.dma("sp", xs[s][:], xT[kc * 128:(kc + 1) * 128, PADC + tt * 1024:PADC + (tt + 1) * 1024],
                          writes=[("xs", s)], slot=f"xs{s}")
                    T.op("act", lambda s=s, kc=kc, tt=tt: A.copy(xbs[tt % 2][:, kc, :], xs[s][:]), reads=[("xs", s)], writes=[("xb", tt % 2)])

            def pre1(mp):
                tt = mp // 4
                if mp % 4 == 0 and tt + 1 < NTT:
                    load_x(tt + 1)

            def epi(mi_all, pss, pkey):
                tt = mi_all // 8
                mi = mi_all % 8
                o = osb[mi % 2]
                T.op("act", lambda: A.copy(o[:, 0:512], pss[0][:, :]), reads=[pkey], writes=[("osb", mi % 2)])
                T.op("act", lambda: A.copy(o[:, 512:1024], pss[1][:, :]), reads=[pkey], writes=[("osb", mi % 2)])
                T.dma("sp", projT[mi * 128:(mi + 1) * 128, tt * 1024:(tt + 1) * 1024], o[:],
                      reads=[("osb", mi % 2)], writes=["projT"], slot=f"osb{mi % 2}")

            class _WinRep:
                def __getitem__(self, idx):
                    r, c = idx
                    base = (r.start // 128) % 4
                    return w_in[base * 128:(base + 1) * 128, c]

            load_x(0)
            gemm(T, nc, "g1", _WinRep(), D, 4 * NTT, lambda mp: xbs[(mp // 4) % 2], lambda mp: ("xb", (mp // 4) % 2),
                 [(0, 512), (512, 512)], epi, pre=pre1)
        T.barrier()
        if debug:
            for i in range(8):
                T.dma("sp", dbg_proj[i * 128:(i + 1) * 128, :], projT[i * 128:(i + 1) * 128, :], reads=["projT"], slot="dbg")
            T.barrier()
        if stop_after <= 1:
            return nc

        with ExitStack() as ph:
            pp = ExitStack()
            cur = [ph]

            def tl(name, shape, dt=F32):
                return _sb(cur[0], nc, "s_" + name, shape, dt)

            Wb = tl("Wb", [16, 16, 128])
            NR = 13
            pwr = tl("pwr", [128, NR, 16]); pwi = tl("pwi", [128, NR, 16])
            jm = tl("jm", [128, 128]); Wc = tl("Wc", [128, 16, 16]); dcol = tl("dcol", [16, 16])
            pp.__enter__()
            cur[0] = pp

            def ew(e, fn, reads, writes):
                T.op(e, fn, reads=reads, writes=writes)

            def ab_calc(P, Fd, lr_d, li_d, ls_d, pfx):
                lr = tl(pfx + "lr", [P, Fd]); li = tl(pfx + "li", [P, Fd]); ls = tl(pfx + "ls", [P, Fd])
                T.dma("sp", lr[:], lr_d[:, :], writes=[pfx + "lr"], slot=pfx + "l1")
                T.dma("sp", li[:], li_d[:, :], writes=[pfx + "li"], slot=pfx + "l2")
                T.dma("sp", ls[:], ls_d[:, :], writes=[pfx + "ls"], slot=pfx + "l3")
                step = tl(pfx + "step", [P, Fd]); mag = tl(pfx + "mag", [P, Fd]); phs = tl(pfx + "ph", [P, Fd])
                m1 = tl(pfx + "m1", [P, Fd]); sn = tl(pfx + "sn", [P, Fd]); cs = tl(pfx + "cs", [P, Fd])
                abr = tl(pfx + "abr", [P, Fd]); abi = tl(pfx + "abi", [P, Fd])
                k = lambda n: pfx + n
                ew("act", lambda: A.activation(out=step[:], in_=ls[:], func=AF.Exp), [k("ls")], [k("step")])
                ew("dve", lambda: V.tensor_tensor(mag[:], lr[:], step[:], ALU.mult), [k("lr"), k("step")], [k("mag")])
                ew("act", lambda: A.activation(out=mag[:], in_=mag[:], func=AF.Exp), [k("mag")], [k("mag")])
                ew("dve", lambda: V.tensor_tensor(phs[:], li[:], step[:], ALU.mult), [k("li"), k("step")], [k("ph")])
                m2 = tl(pfx + "m2", [P, Fd])

                def wrap(shift):
                    ew("dve", lambda: V.tensor_scalar_add(m1[:], phs[:], shift), [k("ph"), k("sn"), k("cs")], [k("m1")])
                    for _ in range(4):
                        ew("dve", lambda: V.tensor_scalar(m2[:], m1[:], PI, -2 * PI, ALU.is_gt, ALU.mult), [k("m1")], [k("m2")])
                        ew("dve", lambda: V.tensor_tensor(m1[:], m1[:], m2[:], ALU.add), [k("m1"), k("m2")], [k("m1")])
                wrap(0.0)
                ew("act", lambda: A.activation(out=sn[:], in_=m1[:], func=AF.Sin), [k("m1")], [k("sn")])
                wrap(0.5 * PI)
                ew("act", lambda: A.activation(out=cs[:], in_=m1[:], func=AF.Sin), [k("m1")], [k("cs")])
                ew("dve", lambda: V.tensor_tensor(abr[:], mag[:], cs[:], ALU.mult), [k("mag"), k("cs")], [k("abr")])
                ew("dve", lambda: V.tensor_tensor(abi[:], mag[:], sn[:], ALU.mult), [k("mag"), k("sn")], [k("abi")])
                return lr, li, abr, abi

            lr, li, abr, abi = ab_calc(16, 1024, lrA, liA, lsA, "A")
            br = tl("br", [16, 1024]); bi = tl("bi", [16, 1024])
            T.dma("sp", br[:], brA[:, :], writes=["br"], slot="Abr")
            T.dma("sp", bi[:], biA[:, :], writes=["bi"], slot="Abi")
            nr = tl("nr", [16, 1024]); den = tl("den", [16, 1024]); t1 = tl("t1", [16, 1024]); t2 = tl("t2", [16, 1024])
            zr = tl("zr", [16, 1024]); zi = tl("zi", [16, 1024])
            ew("dve", lambda: V.tensor_scalar_add(nr[:], abr[:], -1.0), ["Aabr"], ["nr"])
            ew("dve", lambda: V.tensor_tensor(den[:], lr[:], lr[:], ALU.mult), ["Alr"], ["den"])
            ew("dve", lambda: V.tensor_tensor(t1[:], li[:], li[:], ALU.mult), ["Ali"], ["t1"])
            ew("dve", lambda: V.tensor_tensor(den[:], den[:], t1[:], ALU.add), ["den", "t1"], ["den"])
            ew("dve", lambda: V.reciprocal(den[:], den[:]), ["den"], ["den"])
            ew("dve", lambda: V.tensor_tensor(t1[:], nr[:], lr[:], ALU.mult), ["nr", "Alr", "den"], ["t1"])
            ew("dve", lambda: V.tensor_tensor(t2[:], abi[:], li[:], ALU.mult), ["Aabi", "Ali"], ["t2"])
            ew("dve", lambda: V.tensor_tensor(t1[:], t1[:], t2[:], ALU.add), ["t1", "t2"], ["t1"])
            ew("dve", lambda: V.tensor_tensor(zr[:], t1[:], den[:], ALU.mult), ["t1", "den"], ["zr"])
            ew("dve", lambda: V.tensor_tensor(t1[:], abi[:], lr[:], ALU.mult), ["Aabi", "Alr", "zr"], ["t1"])
            ew("dve", lambda: V.tensor_tensor(t2[:], nr[:], li[:], ALU.mult), ["nr", "Ali", "zr"], ["t2"])
            ew("dve", lambda: V.tensor_tensor(t1[:], t1[:], t2[:], ALU.subtract), ["t1", "t2"], ["t1"])
            ew("dve", lambda: V.tensor_tensor(zi[:], t1[:], den[:], ALU.mult), ["t1", "den"], ["zi"])
            v3 = lambda t: t[:].rearrange("p (g n) -> p g n", g=16)
            ew("dve", lambda: V.tensor_tensor(t1[:], zr[:], br[:], ALU.mult), ["zr", "br", "zi"], ["t1"])
            ew("dve", lambda: V.tensor_tensor(t2[:], zi[:], bi[:], ALU.mult), ["zi", "bi"], ["t2"])
            ew("dve", lambda: V.tensor_tensor(Wb[:, :, 0:64], v3(t1), v3(t2), ALU.subtract), ["t1", "t2"], ["Wb"])
            ew("dve", lambda: V.tensor_tensor(t1[:], zr[:], bi[:], ALU.mult), ["zr", "bi", "Wb"], ["t1"])
            ew("dve", lambda: V.tensor_tensor(t2[:], zi[:], br[:], ALU.mult), ["zi", "br", "Wb"], ["t2"])
            ew("dve", lambda: V.tensor_tensor(Wb[:, :, 64:128], v3(t1), v3(t2), ALU.add), ["t1", "t2"], ["Wb"])
            _, _, bar, bai = ab_calc(128, 16, lrB, liB, lsB, "B")
            q1 = tl("q1", [128, 16]); q2 = tl("q2", [128, 16])
            ew("dve", lambda: V.tensor_copy(pwr[:, 0, :], bar[:]), ["Babr"], ["pw"])
            ew("dve", lambda: V.tensor_copy(pwi[:, 0, :], bai[:]), ["Babi"], ["pw"])
            for r in range(1, NR):
                ew("dve", lambda r=r: V.tensor_tensor(q1[:], pwr[:, r - 1, :], pwr[:, r - 1, :], ALU.mult), ["pw"], ["q1"])
                ew("dve", lambda r=r: V.tensor_tensor(q2[:], pwi[:, r - 1, :], pwi[:, r - 1, :], ALU.mult), ["pw"], ["q2"])
                ew("dve", lambda r=r: V.tensor_tensor(pwr[:, r, :], q1[:], q2[:], ALU.subtract), ["q1", "q2"], ["pw"])
                ew("dve", lambda r=r: V.tensor_tensor(q1[:], pwr[:, r - 1, :], pwi[:, r - 1, :], ALU.mult), ["pw"], ["q1"])
                ew("dve", lambda r=r: V.tensor_scalar_mul(pwi[:, r, :], q1[:], 2.0), ["q1"], ["pw"])
            T.dma("sp", jm[:], jmat_in[:, :], writes=["jm"], slot="cjm")
            T.dma("sp", Wc[:].rearrange("p g h -> p (g h)"), cT[:, :], writes=["Wc"], slot="cwc")
            ew("dve", lambda: V.tensor_scalar_mul(Wc[64:128, :, :], Wc[64:128, :, :], -1.0), ["Wc"], ["Wc"])
            T.dma("sp", dcol[:], dT[:, :], writes=["dcol"], slot="cdc")
            T.barrier()
            pp.__exit__(None, None, None)
            cur[0] = ph

            H = [tl("H0", [128, S]), tl("H1", [128, S])]
            ug = tl("ug", [16, S])
            ysb = tl("ysb", [16, 512]); g1 = tl("g1", [16, 512]); g2 = tl("g2", [16, 512]); yb = tl("yb", [16, S], BF16)
            Rt = [tl("Rt0", [128, 128]), tl("Rt1", [128, 128])]
            pss = [_ps(ph, nc, f"s_ps{i}", [128, 512]) for i in range(4)]
            NJ = S // 512
            pi = 0
            for g in range(16):
                T.dma("sp", ug[:], projT[16 * g:16 * g + 16, :], reads=["projT"], writes=["ug"], slot="ug")
                for j in range(NJ):
                    p = pss[pi % 4]; pk = ("sps", pi % 4); pi += 1
                    T.group("pe", [lambda p=p, j=j, g=g: nc.tensor.matmul(p[:, :], Wb[:, g, :], ug[:, j * 512:(j + 1) * 512], start=True, stop=True)],
                            reads=["Wb", "ug"], writes=[pk])
                    T.op("act", lambda p=p, j=j: A.copy(H[0][:, j * 512:(j + 1) * 512], p[:, :]), reads=[pk], writes=[("H", 0, j)])
                for r in range(NR):
                    d = 1 << r
                    src = H[r % 2]; dst = H[(r + 1) % 2]; sb_ = r % 2; db_ = (r + 1) % 2
                    R = Rt[r % 2]; rk = ("Rt", r % 2)
                    T.op("dve", lambda R=R, r=r, g=g: V.tensor_scalar_mul(R[:], ident[:], pwr[:, r, g:g + 1]), reads=["ident", "pw"], writes=[rk])
                    T.op("dve", lambda R=R, r=r, g=g: V.scalar_tensor_tensor(R[:], jm[:], pwi[:, r, g:g + 1], R[:], ALU.mult, ALU.add),
                         reads=["jm", "pw", rk], writes=[rk])
                    for j in range(NJ):
                        lo = j * 512; hi = lo + 512
                        if hi <= d:
                            T.op("pool", lambda lo=lo, hi=hi, src=src, dst=dst: nc.gpsimd.tensor_copy(dst[:, lo:hi], src[:, lo:hi]),
                                 reads=[("H", sb_, j)], writes=[("H", db_, j)])
                            continue
                        s0 = max(lo, d)
                        rj = sorted(set([(s0 - d) // 512, (hi - d - 1) // 512]))
                        p = pss[pi % 4]; pk = ("sps", pi % 4); pi += 1
                        T.group("pe", [lambda p=p, s0=s0, lo=lo, hi=hi, d=d, R=R, src=src: nc.tensor.matmul(
                            p[:, s0 - lo:512], R[:], src[:, s0 - d:hi - d], start=True, stop=True)],
                            reads=[rk] + [("H", sb_, x) for x in rj], writes=[pk])
                        T.op("dve", lambda p=p, s0=s0, lo=lo, hi=hi, src=src, dst=dst: V.tensor_tensor(
                            dst[:, s0:hi], p[:, s0 - lo:512], src[:, s0:hi], ALU.add),
                            reads=[pk, ("H", sb_, j)], writes=[("H", db_, j)])
                        if s0 > lo:
                            T.op("pool", lambda lo=lo, s0=s0, src=src, dst=dst: nc.gpsimd.tensor_copy(dst[:, lo:s0], src[:, lo:s0]),
                                 reads=[("H", sb_, j)], writes=[("H", db_, j)])
                fb = NR % 2
                for j in range(NJ):
                    p = pss[pi % 4]; pk = ("sps", pi % 4); pi += 1
                    T.group("pe", [lambda p=p, j=j, g=g: nc.tensor.matmul(p[0:16, :], Wc[:, g, :], H[fb][:, j * 512:(j + 1) * 512], start=True, stop=True)],
                            reads=["Wc", ("H", fb, j)], writes=[pk])
                    T.op("dve", lambda p=p, j=j, g=g: V.scalar_tensor_tensor(ysb[:], ug[:, j * 512:(j + 1) * 512],
                                                                            dcol[:, g:g + 1], p[0:16, :], ALU.mult, ALU.add),
                         reads=[pk, "ug", "dcol"], writes=["ysb"])
                    ew("dve", lambda: V.tensor_tensor(g1[:], ysb[:], ysb[:], ALU.mult), ["ysb"], ["g1"])
                    ew("dve", lambda: V.tensor_scalar(g1[:], g1[:], 0.044715, 1.0, ALU.mult, ALU.add), ["g1"], ["g1"])
                    ew("dve", lambda: V.tensor_tensor(g1[:], g1[:], ysb[:], ALU.mult), ["g1", "ysb"], ["g1"])
                    ew("act", lambda: A.activation(out=g2[:], in_=g1[:], func=AF.Sigmoid, scale=1.5957691216057308), ["g1"], ["g2"])
                    ew("dve", lambda j=j: V.tensor_tensor(yb[:, j * 512:(j + 1) * 512], g2[:], ysb[:], ALU.mult), ["g2", "ysb"], ["yb"])
                T.dma("sp", ag_in[16 * g:16 * g + 16, PADC:], yb[:], reads=["yb"], writes=["ag_in"], slot="yb")
        T.barrier()

        if stop_after <= 2:
            if debug:
                for i in range(4):
                    T.dma("sp", dbg_ag[i * 128:(i + 1) * 128, :], ag_in[i * 128:(i + 1) * 128, :], reads=["ag_in"], slot="dbg")
                T.barrier()
            return nc
        with ExitStack() as ph:
            def tl(name, shape, dt=F32):
                return _sb(ph, nc, "a_" + name, shape, dt)
            QT = tl("QT", [128, 2, S], BF16); KT = tl("KT", [128, 2, S], BF16)
            Vt = tl("Vt", [128, 64, 264], BF16)
            stg = [tl(f"stg{i}", [128, 2048]) for i in range(2)]
            vtb = tl("vtb", [128, 2048], BF16)
            tp = [_ps(ph, nc, f"a_tp{i}", [128, 128], BF16) for i in range(2)]
            si = 0
            for ci, dstT in ((0, QT), (1, QT), (2, KT), (3, KT)):
                row0 = 256 + ci * 128
                for c4 in range(S // 2048):
                    s = si % 2; si += 1
                    T.dma("sp", stg[s][:], projT[row0:row0 + 128, c4 * 2048:(c4 + 1) * 2048], reads=["projT"], writes=[("stg", s)], slot=f"stg{s}")
                    T.op("act", lambda s=s, dstT=dstT, ci=ci, c4=c4: A.copy(dstT[:, ci % 2, c4 * 2048:(c4 + 1) * 2048], stg[s][:]),
                         reads=[("stg", s)], writes=["QK"])
            T.op("dve", lambda: V.memset(Vt[:, :, 256:257], 1.0), writes=["Vt"])
            ti = 0
            for half in range(2):
                for c4 in range(S // 2048):
                    s = si % 2; si += 1
                    T.dma("sp", stg[s][:], projT[768 + half * 128:768 + (half + 1) * 128, c4 * 2048:(c4 + 1) * 2048], reads=["projT"], writes=[("stg", s)], slot=f"stg{s}")
                    T.op("act", lambda s=s: A.copy(vtb[:], stg[s][:]), reads=[("stg", s)], writes=["vtb"])
                    for b in range(16):
                        t = tp[ti % 2]; tk = ("tp", ti % 2); ti += 1
                        blk = c4 * 16 + b
                        T.group("pe", [lambda t=t, b=b: nc.tensor.transpose(t[:, :], vtb[:, b * 128:(b + 1) * 128], identb[:])],
                                reads=["vtb", "identb"], writes=[tk])
                        T.op("dve", lambda t=t, blk=blk, half=half: V.tensor_copy(Vt[:, blk, half * 128:(half + 1) * 128], t[:, :]),
                             reads=[tk], writes=["Vt"])
            la = tl("la", [128, 128]); lb = tl("lb", [128, 128]); e1 = tl("e1", [128, 1]); e2 = tl("e2", [128, 1]); nlam = tl("nlam", [128, 1])
            for (qa, ka, eo, nm) in ((lq1, lk1, e1, "e1"), (lq2, lk2, e2, "e2")):
                T.dma("sp", la[:], qa[:, :], writes=["la"], slot="la")
                T.dma("sp", lb[:], ka[:, :], writes=["lb"], slot="la")
                T.op("dve", lambda: V.tensor_tensor(la[:], la[:], lb[:], ALU.mult), reads=["la", "lb"], writes=["la"])
                T.op("dve", lambda eo=eo: V.reduce_sum(eo[:], la[:], axis=AX.X), reads=["la"], writes=[nm])
                T.op("act", lambda eo=eo: A.activation(out=eo[:], in_=eo[:], func=AF.Exp), reads=[nm], writes=[nm])
            T.op("dve", lambda: V.tensor_tensor(nlam[:], e2[:], e1[:], ALU.subtract), reads=["e1", "e2"], writes=["nlam"])
            T.op("dve", lambda: V.tensor_scalar_add(nlam[:], nlam[:], -LAMBDA_INIT), reads=["nlam"], writes=["nlam"])
            gb = tl("gb", [128, 256]); cf = tl("cf", [128, 1]); bn = tl("bn", [128, 3, 512])
            T.dma("sp", gb[:], gsub[:, :], writes=["gb"], slot="gb")
            T.dma("sp", cf[:], cfar[:, :], writes=["cf"], slot="cfs")
            for i in range(3):
                T.dma("sp", bn[:, i, :], bnear[i, :, :], writes=["bn"], slot="bns")
            T.op("dve", lambda: V.tensor_scalar_mul(gb[:], gb[:], 1.0 - LAMBDA_INIT), reads=["gb"], writes=["gb"])

            sps = [_ps(ph, nc, f"a_s{i}", [128, 512]) for i in range(2)]
            acc = [[_ps(ph, nc, f"a_acc{c}{u}", [128, 512]) for u in range(2)] for c in range(2)]
            PT = [tl(f"PT{i}", [128, 512], BF16) for i in range(2)]
            tmpb = tl("tmpb", [128, 512])
            o = tl("o", [128, 256]); ob = tl("ob", [128, 256], BF16); sq = tl("sq", [128, 256])
            r1 = tl("r1", [128, 1]); r2 = tl("r2", [128, 1]); ss = tl("ss", [128, 1])
            yT = [tl(f"yT{i}", [128, 256], BF16) for i in range(2)]
            scale = 128.0 ** -0.5
            ui = 0
            for g in range(S // 256):
                q0 = g * 256
                nkb = 2 * g + 2
                for kb in range(nkb):
                    s = ui % 2; ui += 1
                    sp_ = sps[s]; sk = ("sps", s)
                    T.group("pe", [lambda c=c, sp_=sp_, kb=kb, q0=q0: nc.tensor.matmul(sp_[:, c * 256:(c + 1) * 256], KT[:, c, kb * 128:(kb + 1) * 128],
                                                                                     QT[:, c, q0:q0 + 256], start=True, stop=True) for c in range(2)],
                            reads=["QK"], writes=[sk])
                    near = kb - (2 * g - 1)
                    if near >= 0:
                        T.op("dve", lambda sp_=sp_, near=near: V.scalar_tensor_tensor(tmpb[:], sp_[:, :], scale, bn[:, near, :], ALU.mult, ALU.add),
                             reads=[sk, "bn"], writes=["tmpb"])
                        T.op("act", lambda s=s: A.activation(out=PT[s][:], in_=tmpb[:], func=AF.Exp), reads=["tmpb"], writes=[("PT", s)])
                    else:
                        T.op("act", lambda s=s, sp_=sp_: A.activation(out=PT[s][:], in_=sp_[:, :], func=AF.Exp, bias=cf[:, 0:1], scale=scale),
                             reads=[sk, "cf"], writes=[("PT", s)])
                    T.group("pe", [lambda c=c, u=u, s=s, kb=kb, nkb=nkb: nc.tensor.matmul(acc[c][u][:, 0:257], PT[s][:, c * 256 + u * 128:c * 256 + (u + 1) * 128],
                                                                                       Vt[:, kb, 0:257], start=(kb == 0), stop=(kb == nkb - 1))
                                   for c in range(2) for u in range(2)],
                            reads=[("PT", s), "Vt"], writes=["acc"])
                for u in range(2):
                    a1 = acc[0][u]; a2 = acc[1][u]
                    T.op("dve", lambda a1=a1: V.reciprocal(r1[:], a1[:, 256:257]), reads=["acc"], writes=["r1"])
                    T.op("dve", lambda a2=a2: V.reciprocal(r2[:], a2[:, 256:257]), reads=["acc"], writes=["r2"])
                    T.op("dve", lambda: V.tensor_tensor(r2[:], r2[:], nlam[:], ALU.mult), reads=["r2", "nlam"], writes=["r2"])
                    T.op("dve", lambda a1=a1: V.tensor_scalar_mul(o[:], a1[:, 0:256], r1[:, 0:1]), reads=["acc", "r1"], writes=["o"])
                    T.op("dve", lambda a2=a2: V.scalar_tensor_tensor(o[:], a2[:, 0:256], r2[:, 0:1], o[:], ALU.mult, ALU.add), reads=["acc", "r2", "o"], writes=["o"])
                    T.op("dve", lambda: V.tensor_tensor(sq[:], o[:], o[:], ALU.mult), reads=["o"], writes=["sq"])
                    T.op("dve", lambda: V.reduce_sum(ss[:], sq[:], axis=AX.X), reads=["sq"], writes=["ss"])
                    T.op("dve", lambda: V.tensor_scalar(ss[:], ss[:], 1.0 / 256.0, EPS, ALU.mult, ALU.add), reads=["ss"], writes=["ss"])
                    T.op("act", lambda: A.activation(out=ss[:], in_=ss[:], func=AF.Sqrt), reads=["ss"], writes=["ss"])
                    T.op("dve", lambda: V.reciprocal(ss[:], ss[:]), reads=["ss"], writes=["ss"])
                    T.op("dve", lambda: V.scalar_tensor_tensor(ob[:], o[:], ss[:, 0:1], gb[:], ALU.mult, ALU.mult), reads=["o", "ss", "gb"], writes=["ob"])
                    for hf in range(2):
                        t = tp[ti % 2]; tk = ("tp", ti % 2); ti += 1
                        T.group("pe", [lambda t=t, hf=hf: nc.tensor.transpose(t[:, :], ob[:, hf * 128:(hf + 1) * 128], identb[:])],
                                reads=["ob", "identb"], writes=[tk])
                        T.op("act", lambda t=t, hf=hf, u=u: A.copy(yT[hf][:, u * 128:(u + 1) * 128], t[:, :]), reads=[tk], writes=[("yT", hf)])
                for hf in range(2):
                    T.dma("sp", ag_in[256 + hf * 128:256 + (hf + 1) * 128, PADC + q0:PADC + q0 + 256], yT[hf][:],
                          reads=[("yT", hf)], writes=["ag_in"], slot=f"yT{hf}")
        T.barrier()
        if debug:
            for i in range(4):
                T.dma("sp", dbg_ag[i * 128:(i + 1) * 128, :], ag_in[i * 128:(i + 1) * 128, :], reads=["ag_in"], slot="dbg")
            T.barrier()

        if phase_c:
            T._deps("pool", ["ag_in"], ["ag_out"])
            ccs = T._mk("CC")
            nc.gpsimd.collective_compute("AllGather", ALU.bypass, replica_groups=[list(range(NCORE))],
                                         ins=[ag_in.ap().opt()], outs=[ag_out.ap().opt()]).then_inc(T.sem[ccs], 1)
            T.cnt[ccs] += 1
            T._record((ccs, T.cnt[ccs]), ["ag_in"], ["ag_out"])
            T.barrier()
            build_phase_c(nc, T, dr, ag_out, agsel, h1s, p2s, outT, ones, pc_level, dbgc)
        T.barrier()
    return nc


def build_phase_c(nc, T, dr, ag_out, agsel, h1s, p2s, outT, ones, pc_level=3, dbgc=None, sub=9):
    V = nc.vector
    A = nc.scalar
    with ExitStack() as pc:
        def cl(name, shape, dt=F32):
            return _sb(pc, nc, "c_" + name, shape, dt)
        bglu = cl("bglu", [128, 16]); l1g = cl("l1g", [128, 32]); l1b = cl("l1b", [128, 32])
        l2g = cl("l2g", [128, 32]); l2b = cl("l2b", [128, 32]); cwt = cl("cwt", [128, 3 * 86]); cbt = cl("cbt", [128, 86]); hmt = cl("hmt", [128, NTILE])
        for t_, d_ in ((bglu, "b_glu"), (l1g, "ln1g"), (l1b, "ln1b"), (l2g, "ln2g"), (l2b, "ln2b"), (cwt, "cw"), (cbt, "cb"), (hmt, "hm")):
            T.dma("sp", t_[:], dr[d_][:, :], writes=["cparams"], slot="cpar")
        R1 = cl("R1", [128, 32, NTP], BF16)
        oht = cl("oht", [128, NCORE])
        T.dma("sp", oht[:], dr["oh"][:, :], writes=["oht"], slot="ohs")
        with ExitStack() as ph:
            sacc = _sb(ph, nc, "sel_acc", [128, 32, 1024 + PADC], BF16)
            win = [_sb(ph, nc, f"sel_w{i}", [128, 1024 + PADC], BF16) for i in range(2)]
            T.op("dve", lambda: V.memset(sacc[:], 0.0), writes=["sacc"])
            it = 0
            for r in range(NCORE):
                for ch in range(32):
                    s = it % 2; it += 1
                    T.dma("pool", win[s][:], ag_out[ch * 128:(ch + 1) * 128, 1024 * r:1024 * r + 1024 + PADC],
                          reads=["ag_out"], writes=[("win", s)], slot=f"win{s}")
                    T.op("dve", lambda s=s, ch=ch, r=r: V.scalar_tensor_tensor(sacc[:, ch, :], win[s][:], oht[:, r:r + 1], sacc[:, ch, :], ALU.mult, ALU.add),
                         reads=[("win", s), "oht", "sacc"], writes=["sacc"])
            for ch in range(32):
                T.dma("sp", agsel[ch * 128:(ch + 1) * 128, :], sacc[:, ch, :], reads=["sacc"], writes=["agsel"], slot="selout")
            T.barrier()
            if pc_level == 1:
                for ch in range(32):
                    T.dma("sp", dbgc[ch * 128:(ch + 1) * 128, :], agsel[ch * 128:(ch + 1) * 128, :], slot="dbgc")
                T.barrier()
                return
        for tt in range(NTILE):
            with ExitStack() as ph:
                ygel = _sb(ph, nc, "c_ygel", [128, 16, NTP], BF16)
                pre = _sb(ph, nc, "c_pre", [128, 32, NT], F32)
                s1 = _sb(ph, nc, "c_s1", [128, NT], F32); s2 = _sb(ph, nc, "c_s2", [128, NT], F32)
                tq = _sb(ph, nc, "c_tq", [128, NT], F32)
                xr = [_sb(ph, nc, f"c_xr{i}", [128, NT], F32) for i in range(2)]
                sgt = _sb(ph, nc, "c_sg", [128, NT], F32)
                c0 = tt * TT
                for kc in range(16):
                    r = kc // 2
                    T.dma("pool", ygel[:, kc, 0:NT], agsel[512 * r + (kc % 2) * 128:512 * r + (kc % 2) * 128 + 128, c0:c0 + NT],
                          reads=["agsel"], writes=["ygel"], slot="ygld")
                    T.dma("pool", R1[:, 16 + kc, 0:NT], agsel[512 * r + 256 + (kc % 2) * 128:512 * r + 256 + (kc % 2) * 128 + 128, c0:c0 + NT],
                          reads=["agsel"], writes=["R1"], slot="r1ld")
                tok = [(0, 264), (264, 250)]
                if sub <= 1:
                    T.barrier()
                    return

                def epi_glu(mi, pss, pkey):
                    for si, (t0, ln) in enumerate(tok):
                        T.op("act", lambda si=si, t0=t0, ln=ln: A.activation(out=sgt[:, t0:t0 + ln], in_=pss[si][:, 0:ln], func=AF.Sigmoid, bias=bglu[:, mi:mi + 1]),
                             reads=[pkey, "cparams"], writes=["sgt"])
                    T.op("dve", lambda: V.tensor_tensor(R1[:, mi, 0:NT], sgt[:], ygel[:, mi, 0:NT], ALU.mult), reads=["sgt", "ygel"], writes=["R1"])
                gemm(T, nc, "gG", dr["w_glu"], 2048, 8, ygel, "ygel", tok, epi_glu)
                if sub <= 2:
                    T.barrier()
                    return
                T.op("dve", lambda: V.memset(s1[:], 0.0), writes=["s1"])
                T.op("dve", lambda: V.memset(s2[:], 0.0), writes=["s2"])

                def epi_out(mi, pss, pkey):
                    x_ = xr[mi % 2]
                    T.dma("pool", x_[:], dr["xTc"][mi * 128:(mi + 1) * 128, c0:c0 + NT], writes=[("xr", mi % 2)], slot=f"xr{mi % 2}")
                    for si, (t0, ln) in enumerate(tok):
                        T.op("dve", lambda si=si, t0=t0, ln=ln: V.scalar_tensor_tensor(pre[:, mi, t0:t0 + ln], x_[:, t0:t0 + ln], ALPHA, pss[si][:, 0:ln], ALU.mult, ALU.add),
                             reads=[pkey, ("xr", mi % 2)], writes=["pre"])
                    T.op("pool", lambda: nc.gpsimd.tensor_tensor(s1[:], s1[:], pre[:, mi, :], ALU.add), reads=["pre", "s1"], writes=["s1"])
                    T.op("act", lambda: A.activation(out=tq[:], in_=pre[:, mi, :], func=AF.Square), reads=["pre"], writes=["tq"])
                    T.op("pool", lambda: nc.gpsimd.tensor_tensor(s2[:], s2[:], tq[:], ALU.add), reads=["tq", "s2"], writes=["s2"])
                gemm(T, nc, "gO", dr["w_out"], D, 16, R1, "R1", tok, epi_out)
                T.barrier()
                if sub <= 3:
                    return
                ln_finish(nc, T, ph, s1, s2, tok, ones, "l1")
                mean_b, rstd_b = s1, s2
                if sub <= 4:
                    T.barrier()
                    return
                for mi in range(32):
                    T.op("dve", lambda mi=mi: V.tensor_tensor(pre[:, mi, :], pre[:, mi, :], mean_b[:], ALU.subtract), reads=["pre", "s1"], writes=["pre"])
                    T.op("dve", lambda mi=mi: V.tensor_tensor(pre[:, mi, :], pre[:, mi, :], rstd_b[:], ALU.mult), reads=["pre", "s2"], writes=["pre"])
                    T.op("dve", lambda mi=mi: V.tensor_scalar(pre[:, mi, :], pre[:, mi, :], l1g[:, mi:mi + 1], l1b[:, mi:mi + 1], ALU.mult, ALU.add),
                         reads=["pre", "cparams"], writes=["pre"])
                    T.op("act", lambda mi=mi: A.copy(R1[:, mi, 0:NT], pre[:, mi, :]), reads=["pre"], writes=["R1"])
                    T.op("dve", lambda mi=mi: V.tensor_scalar_mul(R1[:, mi, 0:PADC], R1[:, mi, 0:PADC], hmt[:, tt:tt + 1]), reads=["R1", "cparams"], writes=["R1"])
                    T.dma("sp", h1s[mi * 128:(mi + 1) * 128, :], pre[:, mi, PADC:], reads=["pre"], writes=["h1s"], slot="h1s")
                T.barrier()
                if pc_level == 2:
                    for ch in range(32):
                        T.dma("sp", dbgc[ch * 128:(ch + 1) * 128, :], h1s[ch * 128:(ch + 1) * 128, :], slot="dbgc")
                    T.barrier()
                    return
            with ExitStack() as ph:
                hff = _sb(ph, nc, "c_hff", [128, 86, TT], BF16)
                asb = _sb(ph, nc, "c_asb", [128, TT], F32)
                gs = _sb(ph, nc, "c_gs", [128, NT], F32)
                t1 = _sb(ph, nc, "c_t1", [128, TT], F32)
                sg = _sb(ph, nc, "c_sgl", [128, TT], F32)
                tok = [(0, 264), (264, 250)]
                cols = []
                for f in range(86):
                    cols += [f * 128, DFF + f * 128]

                def epi_up(mi, pss, pkey):
                    f = mi // 2
                    if mi % 2 == 0:
                        T.op("act", lambda: A.copy(asb[:, 0:262], pss[0][:, PADC:264]), reads=[pkey], writes=["asb"])
                        T.op("act", lambda: A.copy(asb[:, 262:TT], pss[1][:, 0:250]), reads=[pkey], writes=["asb"])
                    else:
                        T.op("act", lambda: A.copy(gs[:, 0:264], pss[0][:, 0:264]), reads=[pkey], writes=["gs"])
                        T.op("act", lambda: A.copy(gs[:, 264:NT], pss[1][:, 0:250]), reads=[pkey], writes=["gs"])
                        T.op("dve", lambda: V.tensor_scalar(t1[:], gs[:, 2:NT], cwt[:, 2 * 86 + f:2 * 86 + f + 1], cbt[:, f:f + 1], ALU.mult, ALU.add),
                             reads=["gs", "cparams"], writes=["t1"])
                        T.op("dve", lambda: V.scalar_tensor_tensor(t1[:], gs[:, 1:NT - 1], cwt[:, 86 + f:86 + f + 1], t1[:], ALU.mult, ALU.add),
                             reads=["gs", "t1"], writes=["t1"])
                        T.op("dve", lambda: V.scalar_tensor_tensor(t1[:], gs[:, 0:NT - 2], cwt[:, f:f + 1], t1[:], ALU.mult, ALU.add),
                             reads=["gs", "t1"], writes=["t1"])
                        T.op("act", lambda: A.activation(out=sg[:], in_=t1[:], func=AF.Silu), reads=["t1"], writes=["sg"])
                        T.op("dve", lambda: V.tensor_tensor(hff[:, f, :], sg[:], asb[:], ALU.mult), reads=["sg", "asb"], writes=["hff"])
                gemm(T, nc, "gU", dr["w_up"], D, 86, R1, "R1", tok, epi_up)
                T.barrier()
                pr = [_sb(ph, nc, f"c_pr{i}", [128, TT], F32) for i in range(2)]
                s1 = _sb(ph, nc, "c_s1b", [128, TT], F32); s2 = _sb(ph, nc, "c_s2b", [128, TT], F32)
                tq = _sb(ph, nc, "c_tqb", [128, TT], F32)
                hr = [_sb(ph, nc, f"c_hr{i}", [128, TT], F32) for i in range(2)]
                tok2 = [(0, 256), (256, 256)]
                T.op("dve", lambda: V.memset(s1[:], 0.0), writes=["s1"])
                T.op("dve", lambda: V.memset(s2[:], 0.0), writes=["s2"])

                def epi_dn(mi, pss, pkey):
                    h_ = hr[mi % 2]; p_ = pr[mi % 2]; pk = ("pr", mi % 2)
                    T.dma("pool", h_[:], h1s[mi * 128:(mi + 1) * 128, :], reads=["h1s"], writes=[("hr", mi % 2)], slot=f"hr{mi % 2}")
                    for si, (t0, ln) in enumerate(tok2):
                        T.op("dve", lambda si=si, t0=t0, ln=ln: V.scalar_tensor_tensor(p_[:, t0:t0 + ln], h_[:, t0:t0 + ln], ALPHA, pss[si][:, 0:ln], ALU.mult, ALU.add),
                             reads=[pkey, ("hr", mi % 2)], writes=[pk])
                    T.op("pool", lambda: nc.gpsimd.tensor_tensor(s1[:], s1[:], p_[:], ALU.add), reads=[pk, "s1"], writes=["s1"])
                    T.op("act", lambda: A.activation(out=tq[:], in_=p_[:], func=AF.Square), reads=[pk], writes=["tq"])
                    T.op("pool", lambda: nc.gpsimd.tensor_tensor(s2[:], s2[:], tq[:], ALU.add), reads=["tq", "s2"], writes=["s2"])
                    T.dma("sp", p2s[mi * 128:(mi + 1) * 128, :], p_[:], reads=[pk], writes=["p2s"], slot=f"prs{mi % 2}")
                gemm(T, nc, "gD", dr["w_dn"], DFF, 16, hff, "hff", tok2, epi_dn)
                T.barrier()
                ln_finish(nc, T, ph, s1, s2, tok2, ones, "l2")
                for mi in range(32):
                    p_ = pr[mi % 2]; pk = ("pr", mi % 2)
                    T.dma("pool", p_[:], p2s[mi * 128:(mi + 1) * 128, :], reads=["p2s"], writes=[pk], slot=f"prl{mi % 2}")
                    T.op("dve", lambda p_=p_: V.tensor_tensor(p_[:], p_[:], s1[:], ALU.subtract), reads=[pk, "s1"], writes=[pk])
                    T.op("dve", lambda p_=p_: V.tensor_tensor(p_[:], p_[:], s2[:], ALU.mult), reads=[pk, "s2"], writes=[pk])
                    T.op("dve", lambda p_=p_, mi=mi: V.tensor_scalar(p_[:], p_[:], l2g[:, mi:mi + 1], l2b[:, mi:mi + 1], ALU.mult, ALU.add),
                         reads=[pk, "cparams"], writes=[pk])
                    T.dma("sp", outT[mi * 128:(mi + 1) * 128, tt * TT:(tt + 1) * TT], p_[:], reads=[pk], writes=["outT"], slot=f"outs{mi % 2}")
                T.barrier()


def ln_finish(nc, T, ph, s1, s2, tok, ones, tag):
    V = nc.vector
    pa = [_ps(ph, nc, f"{tag}_pa{i}", [128, 512]) for i in range(2)]
    pb = [_ps(ph, nc, f"{tag}_pb{i}", [128, 512]) for i in range(2)]
    T.group("pe", [lambda si=si, t0=t0, ln=ln: nc.tensor.matmul(pa[si][:, 0:ln], ones[:], s1[:, t0:t0 + ln], start=True, stop=True)
                   for si, (t0, ln) in enumerate(tok)], reads=["ones", "s1"], writes=[tag + "pa"])
    T.group("pe", [lambda si=si, t0=t0, ln=ln: nc.tensor.matmul(pb[si][:, 0:ln], ones[:], s2[:, t0:t0 + ln], start=True, stop=True)
                   for si, (t0, ln) in enumerate(tok)], reads=["ones", "s2"], writes=[tag + "pb"])
    for si, (t0, ln) in enumerate(tok):
        T.op("dve", lambda si=si, t0=t0, ln=ln: V.tensor_scalar_mul(s1[:, t0:t0 + ln], pa[si][:, 0:ln], 1.0 / D), reads=[tag + "pa"], writes=["s1"])
        T.op("dve", lambda si=si, t0=t0, ln=ln: V.tensor_scalar_mul(s2[:, t0:t0 + ln], pb[si][:, 0:ln], 1.0 / D), reads=[tag + "pb"], writes=["s2"])
    tmp = _sb(ph, nc, f"{tag}_tmp", list(s1.shape), F32)
    T.op("dve", lambda: V.tensor_tensor(tmp[:], s1[:], s1[:], ALU.mult), reads=["s1"], writes=[tag + "tmp"])
    T.op("dve", lambda: V.tensor_tensor(s2[:], s2[:], tmp[:], ALU.subtract), reads=["s2", tag + "tmp"], writes=["s2"])
    T.op("dve", lambda: V.tensor_scalar_add(s2[:], s2[:], EPS), reads=["s2"], writes=["s2"])
    T.op("act", lambda: nc.scalar.activation(out=s2[:], in_=s2[:], func=AF.Sqrt), reads=["s2"], writes=["s2"])
    T.op("dve", lambda: V.reciprocal(s2[:], s2[:]), reads=["s2"], writes=["s2"])


def t5_bucket_np(rel):
    half = 16
    max_exact = 8
    ret = np.where(rel > 0, half, 0)
    n = np.abs(rel)
    nf = np.maximum(n, 1).astype(np.float32)
    large = max_exact + (np.log(nf / max_exact) / math.log(128 / max_exact) * (half - max_exact)).astype(np.int32)
    large = np.minimum(large, half - 1)
    return ret + np.where(n < max_exact, n, large)


def pretile(W):
    K, M = W.shape
    return np.ascontiguousarray(W.reshape(K // 128, 128, M // 256, 256).transpose(2, 1, 0, 3)).reshape((M // 256) * 128, (K // 128) * 256)


def host_inputs(inp, phase_c=True):
    f = np.float32
    x = np.asarray(inp["x"], f)[0]
    xT = np.zeros((D, S + PADC), f)
    xT[:, PADC:] = x.T
    w_in = np.asarray(inp["w_in"], f)[0]
    ident = np.eye(128, dtype=f)
    jm = np.zeros((128, 128), f)
    for n in range(64):
        jm[n, 64 + n] = 1.0
        jm[64 + n, n] = -1.0
    lam_re = np.asarray(inp["ssm_lambda_re"], f)[0]; lam_im = np.asarray(inp["ssm_lambda_im"], f)[0]
    ls = np.asarray(inp["ssm_log_step"], f)[0]
    b_re = np.asarray(inp["ssm_b_re"], f)[0]; b_im = np.asarray(inp["ssm_b_im"], f)[0]
    c_re = np.asarray(inp["ssm_c_re"], f)[0]; c_im = np.asarray(inp["ssm_c_im"], f)[0]
    sd = np.asarray(inp["ssm_d"], f)[0]
    rel_bias = np.asarray(inp["rel_bias"], f)
    maps = []
    shared = {}
    if phase_c:
        shared["w_glu"] = pretile(np.asarray(inp["ssm_w_glu"], f)[0])
        shared["w_out"] = pretile(np.asarray(inp["w_out"], f)[0])
        wu = np.asarray(inp["ffn_w_up"], f)[0]
        wui = np.stack([wu[:, :DFF].reshape(D, 86, 128), wu[:, DFF:].reshape(D, 86, 128)], axis=2).reshape(D, 86 * 256)
        shared["w_up"] = pretile(wui)
        del wui
        shared["w_dn"] = pretile(np.asarray(inp["ffn_w_down"], f)[0])
    kk = np.arange(128)[:, None]
    qq = np.arange(256)[None, :]
    for c in range(NCORE):
        m = {}
        m["xT"] = xT
        cols = np.concatenate([np.arange(256 * c, 256 * c + 256), 2048 + np.arange(256 * c, 256 * c + 256),
                               4096 + np.arange(256 * c, 256 * c + 256), 6144 + np.arange(256 * c, 256 * c + 256)])
        m["w_in"] = pretile(w_in[:, cols])
        m["ident"] = ident; m["jmat"] = jm
        gs = slice(16 * c, 16 * c + 16)
        m["lrA"] = np.ascontiguousarray(np.broadcast_to(lam_re[gs].reshape(1, 1024), (16, 1024)))
        m["liA"] = np.ascontiguousarray(np.broadcast_to(lam_im[gs].reshape(1, 1024), (16, 1024)))
        m["lsA"] = np.ascontiguousarray(np.broadcast_to(np.repeat(ls[gs], 64).reshape(1, 1024), (16, 1024)))
        m["brA"] = np.ascontiguousarray(b_re[gs].transpose(2, 0, 1).reshape(16, 1024))
        m["biA"] = np.ascontiguousarray(b_im[gs].transpose(2, 0, 1).reshape(16, 1024))
        m["lrB"] = np.ascontiguousarray(np.concatenate([lam_re[gs].T, lam_re[gs].T], 0))
        m["liB"] = np.ascontiguousarray(np.concatenate([lam_im[gs].T, lam_im[gs].T], 0))
        m["lsB"] = np.ascontiguousarray(np.broadcast_to(ls[gs].reshape(1, 16), (128, 16)))
        m["cT"] = np.ascontiguousarray(np.concatenate([c_re[gs].transpose(2, 0, 1), c_im[gs].transpose(2, 0, 1)], 0).reshape(128, 256))
        m["dT"] = np.ascontiguousarray(sd[256 * c:256 * c + 256].reshape(16, 16).T)
        for nm in ("lq1", "lk1", "lq2", "lk2"):
            key = {"lq1": "att_lambda_q1", "lk1": "att_lambda_k1", "lq2": "att_lambda_q2", "lk2": "att_lambda_k2"}[nm]
            m[nm] = np.ascontiguousarray(np.broadcast_to(np.asarray(inp[key], f)[0].reshape(1, 128), (128, 128)))
        m["gsub"] = np.ascontiguousarray(np.broadcast_to(np.asarray(inp["att_subln_g"], f)[0].reshape(1, 256), (128, 256)))
        m["cfar"] = np.full((128, 1), rel_bias[15, c], f)
        bn = np.zeros((3, 128, 512), f)
        for i in range(3):
            rel = (128 * (i - 1) + kk) - qq
            b = rel_bias[t5_bucket_np(rel), c].astype(f)
            kpos = 128 * (i - 1) + kk
            vis = (kpos // 64) <= (qq // 64)
            b = np.where(vis, b, f(-1e30)).astype(f)
            bn[i, :, 0:256] = b
            bn[i, :, 256:512] = b
        m["bnear"] = bn
        if phase_c:
            m["w_glu"] = shared["w_glu"]
            m["b_glu"] = np.ascontiguousarray(np.asarray(inp["ssm_b_glu"], f)[0].reshape(16, 128).T)
            m["w_out"] = shared["w_out"]
            m["ln1g"] = np.ascontiguousarray(np.asarray(inp["ln1_g"], f)[0].reshape(32, 128).T)
            m["ln1b"] = np.ascontiguousarray(np.asarray(inp["ln1_b"], f)[0].reshape(32, 128).T)
            m["w_up"] = shared["w_up"]
            cwv = np.asarray(inp["ffn_conv_w"], f)[0]
            m["cw"] = np.ascontiguousarray(cwv.reshape(3, 86, 128).transpose(2, 0, 1).reshape(128, 258))
            m["cb"] = np.ascontiguousarray(np.asarray(inp["ffn_conv_b"], f)[0].reshape(86, 128).T)
            m["w_dn"] = shared["w_dn"]
            m["ln2g"] = np.ascontiguousarray(np.asarray(inp["ln2_g"], f)[0].reshape(32, 128).T)
            m["ln2b"] = np.ascontiguousarray(np.asarray(inp["ln2_b"], f)[0].reshape(32, 128).T)
            hmv = np.ones((128, NTILE), f)
            if c == 0:
                hmv[:, 0] = 0.0
            m["hm"] = hmv
            ohv = np.zeros((128, NCORE), f); ohv[:, c] = 1.0
            m["oh"] = ohv
            m["xTc"] = np.ascontiguousarray(xT[:, 1024 * c:1024 * c + 1024 + PADC])
        maps.append(m)
    return maps


def kernel(**inputs):
    nc = build(debug=False, phase_c=True)
    maps = host_inputs(inputs, True)
    res = run_bass_kernel_spmd(nc, maps, core_ids=list(range(NCORE)))
    out = np.concatenate([np.asarray(res.results[c]["outT"], np.float32).T for c in range(NCORE)], axis=0)
    return out.reshape(1, S, D)
```
